# Optimizing a Trainium2 kernel written in Bass

```python
import jax, jax.numpy as jnp
from jax import lax
import numpy as np

D_MODEL = 4096
BATCH = 4
SEQ = 4096
DEPTH = 1

CHUNK = 64
N_LEFT_CHUNKS = 8
BAND = (N_LEFT_CHUNKS + 1) * CHUNK
ATTN_WIDTH = D_MODEL // 2
ATTN_HEAD_DIM = 128
ATTN_HEADS = ATTN_WIDTH // ATTN_HEAD_DIM
MAX_REL = 128
N_REL = 2 * MAX_REL + 1
POOL_WIDTH = D_MODEL // 2
POOL_WINDOWS = (2, 4, 8, 16)
N_POOL_GROUPS = len(POOL_WINDOWS)
POOL_GROUP_DIM = POOL_WIDTH // N_POOL_GROUPS
N_BRANCHES = 2
IN_COLS = 4 * ATTN_WIDTH + 2 * POOL_WIDTH + N_BRANCHES * D_MODEL
EPS = 1e-6

kernel_name = "hybrid_chunk_attn_pool_gated_block"


def _rmsnorm(x, g):
    xf = x.astype(jnp.float32)
    y = xf * lax.rsqrt(jnp.mean(xf * xf, axis=-1, keepdims=True) + EPS)
    return (y * g.astype(jnp.float32)).astype(x.dtype)


def _chunk_band_attention(q, k, v, rel_bias):
    B, S, H, Dh = q.shape
    nc = S // CHUNK
    pad = N_LEFT_CHUNKS * CHUNK
    kp = jnp.pad(k, ((0, 0), (pad, 0), (0, 0), (0, 0)))
    vp = jnp.pad(v, ((0, 0), (pad, 0), (0, 0), (0, 0)))
    qc = q.reshape(B, nc, CHUNK, H, Dh).transpose(1, 0, 2, 3, 4)
    rel = pad + jnp.arange(CHUNK)[:, None] - jnp.arange(BAND)[None, :]
    bias = rel_bias[:, jnp.clip(rel, -MAX_REL, MAX_REL) + MAX_REL].astype(jnp.float32)
    scale = Dh ** -0.5
    band_offsets = jnp.arange(BAND) - pad

    def one_chunk(args):
        c, qb = args
        start = c * CHUNK
        kb = lax.dynamic_slice_in_dim(kp, start, BAND, axis=1)
        vb = lax.dynamic_slice_in_dim(vp, start, BAND, axis=1)
        s = jnp.einsum('bqhd,bkhd->bhqk', qb, kb,
                       preferred_element_type=jnp.float32) * scale + bias
        valid = (start + band_offsets) >= 0
        s = jnp.where(valid[None, None, None, :], s, -jnp.inf)
        p = jax.nn.softmax(s, axis=-1)
        return jnp.einsum('bhqk,bkhd->bqhd', p.astype(vb.dtype), vb)

    out = lax.map(one_chunk, (jnp.arange(nc), qc))
    return out.transpose(1, 0, 2, 3, 4).reshape(B, S, H * Dh)


def _multiscale_pool(u, pool_w, pool_scale):
    B, S, W = u.shape
    uf = u.astype(jnp.float32).reshape(B, S, N_POOL_GROUPS, POOL_GROUP_DIM)
    cs = jnp.pad(jnp.cumsum(uf, axis=1), ((0, 0), (1, 0), (0, 0), (0, 0)))
    t = jnp.arange(S)
    means = []
    for g, w in enumerate(POOL_WINDOWS):
        csg = cs[:, :, g]
        lo = jnp.pad(csg[:, :S + 1 - w], ((0, 0), (w - 1, 0), (0, 0)))
        cnt = jnp.minimum(t + 1, w).astype(jnp.float32)[None, :, None]
        means.append((csg[:, 1:] - lo) / cnt)
    mean = jnp.stack(means, axis=2)
    d = (mean - uf).astype(u.dtype)
    y = jnp.einsum('bsgc,gcd->bsgd', d, pool_w)
    return y.reshape(B, S, W) * pool_scale


def setup_inputs(seed: int = 0) -> dict:
    key = jax.random.key(seed)
    ks = jax.random.split(key, 12)
    f = jnp.float32
    x = jax.random.normal(ks[0], (BATCH, SEQ, D_MODEL), f)
    norm_gain = 1.0 + 0.1 * jax.random.normal(ks[1], (D_MODEL,), f)
    w_in = jax.random.normal(ks[2], (D_MODEL, IN_COLS), f) * D_MODEL ** -0.5
    rel_bias = 0.5 * jax.random.normal(ks[3], (ATTN_HEADS, N_REL), f)
    pool_w = jax.random.normal(ks[4], (N_POOL_GROUPS, POOL_GROUP_DIM, POOL_GROUP_DIM), f) * POOL_GROUP_DIM ** -0.5
    pool_scale = 1.0 + 0.1 * jax.random.normal(ks[5], (POOL_WIDTH,), f)
    w_out_attn = jax.random.normal(ks[6], (ATTN_WIDTH, D_MODEL), f) * ATTN_WIDTH ** -0.5
    w_out_pool = jax.random.normal(ks[7], (POOL_WIDTH, D_MODEL), f) * POOL_WIDTH ** -0.5
    gate_bias = 0.1 * jax.random.normal(ks[8], (N_BRANCHES, D_MODEL), f)
    w_out = jax.random.normal(ks[9], (D_MODEL, D_MODEL), f) * D_MODEL ** -0.5
    final_gain = 1.0 + 0.1 * jax.random.normal(ks[10], (D_MODEL,), f)
    return {"x": x, "norm_gain": norm_gain, "w_in": w_in, "rel_bias": rel_bias,
            "pool_w": pool_w, "pool_scale": pool_scale, "w_out_attn": w_out_attn,
            "w_out_pool": w_out_pool, "gate_bias": gate_bias, "w_out": w_out,
            "final_gain": final_gain}


def reference(x, norm_gain, w_in, rel_bias, pool_w, pool_scale, w_out_attn,
              w_out_pool, gate_bias, w_out, final_gain):
    B, S, D = x.shape
    A, P = ATTN_WIDTH, POOL_WIDTH
    for _ in range(DEPTH):
        h = _rmsnorm(x, norm_gain)
        proj = jnp.einsum('bsd,dn->bsn', h, w_in)
        o = 0
        q = proj[..., o:o + A]; o += A
        k = proj[..., o:o + A]; o += A
        v = proj[..., o:o + A]; o += A
        z_attn = proj[..., o:o + A]; o += A
        u_pool = proj[..., o:o + P]; o += P
        z_pool = proj[..., o:o + P]; o += P
        g_attn = proj[..., o:o + D]; o += D
        g_pool = proj[..., o:o + D]

        hs = (B, S, ATTN_HEADS, ATTN_HEAD_DIM)
        y_attn = _chunk_band_attention(q.reshape(hs), k.reshape(hs), v.reshape(hs), rel_bias)
        y_attn = y_attn * jax.nn.silu(z_attn)
        y_pool = _multiscale_pool(u_pool, pool_w, pool_scale) * jax.nn.silu(z_pool)

        m = (jax.nn.sigmoid(g_attn + gate_bias[0]) * jnp.einsum('bsa,ad->bsd', y_attn, w_out_attn)
             + jax.nn.sigmoid(g_pool + gate_bias[1]) * jnp.einsum('bsp,pd->bsd', y_pool, w_out_pool))
        x = x + jnp.einsum('bsd,de->bse', m, w_out)
    return _rmsnorm(x, final_gain)
```

```python
import numpy as np
import concourse.bass as bass
import concourse.mybir as mybir
from concourse.bass_utils import run_bass_kernel_spmd

F32 = mybir.dt.float32
BF16 = mybir.dt.bfloat16
AF = mybir.ActivationFunctionType
ALU = mybir.AluOpType

D = 4096
T = 512
NTILE = 4
OWN = T * NTILE
AW = 2048
IN_COLS = 20480
C_Q, C_K, C_V, C_ZA, C_U, C_ZP, C_GA, C_GP = 0, 2048, 4096, 6144, 8192, 10240, 12288, 16384
EPS = 1e-6
SCALE = 128 ** -0.5
POOL_W = (2, 4, 8, 16)
NSLOT = 4
CACHE_POLICY = "none"
SLABS_P12 = 100
SLABS_TILE = 228


class Buf:
    __slots__ = ("w", "r")

    def __init__(self, inherit=()):
        self.w = None
        self.r = []
        for b in inherit:
            if b.w is not None:
                self.r.append(b.w)
            self.r.extend(b.r)
        self._prune()

    def _prune(self):
        best = {}
        for k, v in self.r:
            if best.get(k, 0) < v:
                best[k] = v
        self.r = list(best.items())


class Stream:
    def __init__(self, name):
        self.name = name
        self.ops = []
        self.count = 0
        self.waited = {}

    def wait(self, tok):
        if tok is None:
            return
        k, v = tok
        if self.name == "pe" and k == "pe":
            return
        if self.waited.get(k, 0) >= v:
            return
        self.waited[k] = v
        self.ops.append(("wait", k, v))

    def emit(self, fn, signal):
        if signal:
            self.count += 1
            self.ops.append(("op", fn, self.name, 1))
            return (self.name, self.count)
        self.ops.append(("op", fn, None, 0))
        return None


class Prog:
    def __init__(self):
        self.s = {n: Stream(n) for n in ("pe", "act", "dve", "pool", "sp")}
        self.dma_count = {}

    def _deps(self, rd, wr, extra):
        deps = list(extra)
        for b in rd:
            deps.append(b.w)
        for b in wr:
            deps.append(b.w)
            deps.extend(b.r)
        return deps

    def _mark(self, tok, rd, wr):
        for b in rd:
            b.r.append(tok)
            if len(b.r) > 12:
                b._prune()
        for b in wr:
            b.w = tok
            b.r = []

    def run(self, eng, fn, rd=(), wr=(), extra=()):
        st = self.s[eng]
        for d in self._deps(rd, wr, extra):
            st.wait(d)
        tok = st.emit(fn, True)
        self._mark(tok, rd, wr)
        return tok

    def pe_group(self, fns, rd=(), wr=(), extra=()):
        st = self.s["pe"]
        for d in self._deps(rd, wr, extra):
            st.wait(d)
        tok = None
        for i, fn in enumerate(fns):
            tok = st.emit(fn, i == len(fns) - 1)
        self._mark(tok, rd, wr)
        return tok

    def dma(self, eng, semkey, fn, rd=(), wr=(), extra=()):
        st = self.s[eng]
        for d in self._deps(rd, wr, extra):
            st.wait(d)
        self.dma_count[semkey] = self.dma_count.get(semkey, 0) + 1
        st.ops.append(("op", fn, semkey, 16))
        tok = (semkey, 16 * self.dma_count[semkey])
        self._mark(tok, rd, wr)
        return tok


def build_program():
    nc = bass.Bass("TRN2", target_bir_lowering=False)
    P = Prog()

    def din(name, shape):
        return nc.dram_tensor(name, list(shape), F32, kind="ExternalInput").ap()

    x_own = din("x_own", [OWN, D])
    x_halo = din("x_halo", [T, D])
    w_in = din("w_in", [D, IN_COLS])
    pool_w = din("pool_w", [2048, 512])
    w_oa = din("w_oa", [AW, D])
    w_op = din("w_op", [AW, D])
    w_out = din("w_out", [D, D])
    ng = din("ng", [D])
    fg = din("fg", [D])
    gb_fm = din("gb_fm", [128, 64])
    ps_fm = din("ps_fm", [128, 16])
    ebias = din("ebias", [16, 128, 256])
    emask = din("emask", [128, 256])
    bfar_d = din("bfar", [128, 16])
    hv_d = din("hv", [128, 128])
    invc_d = din("invc", [128, 64])
    ident_d = din("ident", [128, 128])
    out = nc.dram_tensor("out", [OWN, D], F32, kind="ExternalOutput").ap()
    wcache = nc.dram_tensor("wcache", [SLABS_TILE, 128, 4096], BF16, kind="Internal").ap()

    w_in_v = w_in.rearrange("(kc p) n -> p kc n", p=128)
    w_oa_v = w_oa.rearrange("(kc p) n -> p kc n", p=128)
    w_op_v = w_op.rearrange("(kc p) n -> p kc n", p=128)
    w_out_v = w_out.rearrange("(kc p) n -> p kc n", p=128)
    pool_w_v = pool_w.rearrange("(kc p) n -> p kc n", p=128)

    ctx = (
        nc.sbuf_tensor("sb_regA", [128, 16384], F32),
        nc.sbuf_tensor("sb_regR0", [128, 8192], F32),
        nc.sbuf_tensor("sb_regR1", [128, 8192], F32),
        nc.sbuf_tensor("sb_regC", [128, 6144], F32),
        nc.sbuf_tensor("sb_wring", [128, NSLOT, 4096], BF16),
        nc.sbuf_tensor("sb_etab", [128, 16, 256], F32),
        nc.sbuf_tensor("sb_ident_bf", [128, 128], BF16),
        nc.sbuf_tensor("sb_ones_bf", [128, 128], BF16),
        nc.sbuf_tensor("sb_hv_bf", [128, 128], BF16),
        nc.sbuf_tensor("sb_hb", [128, 64], F32),
        nc.sbuf_tensor("sb_psc", [128, 16], F32),
        nc.sbuf_tensor("sb_bfar", [128, 16], F32),
        nc.sbuf_tensor("sb_invc", [128, 64], F32),
        nc.sbuf_tensor("sb_tails", [128, 16, 16], F32),
        nc.sbuf_tensor("sb_stat", [128, 96], F32),
        nc.sbuf_tensor("sb_httail", [128, 32, 16], BF16),
        nc.psum_tensor("ps", [128, 8, 512], F32),
    )
    sem_names = ["pe", "act", "dve", "pool", "sp", "cst", "xs0", "xs1", "gn", "x40", "x41",
                 "fg0", "fg1", "fg2", "fg3", "out0", "out1", "out2", "out3"] + ["w%d" % i for i in range(NSLOT)] + ["wb%d" % i for i in range(NSLOT)]
    sem_ctx = [nc.semaphore("s_" + n) for n in sem_names]

    from contextlib import ExitStack
    with ExitStack() as es:
        (regA, regR0, regR1, regC, wring, etab, ident_bf, ones_bf, hv_bf, hb, psc, bfar, invc,
         tails, stat, httail, ps) = [es.enter_context(c) for c in ctx]
        sems = {n: es.enter_context(c) for n, c in zip(sem_names, sem_ctx)}
        block = es.enter_context(nc.Block())
        regR = [regR0, regR1]

        hT = regA[:, 0:8192].bitcast(BF16).rearrange("p (a b) -> p a b", a=32)
        yaT = regA[:, 8192:12288].bitcast(BF16).rearrange("p (a b) -> p a b", a=16)
        ypT = regA[:, 12288:16384].bitcast(BF16).rearrange("p (a b) -> p a b", a=16)
        outb = regA[:, :].rearrange("p (a b) -> p a b", a=4)
        gain_bc = regA[:, 8192:12288]
        xn = regA[:, 12288:16384].bitcast(BF16).rearrange("p (a b) -> p a b", a=2)

        def KT(k):
            return regR[k][:, 0:4096].bitcast(BF16).rearrange("p (a b) -> p a b", a=16)

        def VV(k):
            return regR[k][:, 4096:8192].bitcast(BF16).rearrange("p (a b) -> p a b", a=4)

        def MT(k):
            return regR[k][:, :].bitcast(BF16).rearrange("p (a b) -> p a b", a=32)

        def XST(k):
            return regR[k][:, :].rearrange("p (a b) -> p a b", a=2)

        def slot16(s):
            return wring[:, s, :].rearrange("p (a b) -> p a b", a=16)

        def slot8(s):
            return wring[:, s, :].rearrange("p (a b) -> p a b", a=8)

        def bank(b):
            return ps[:, b, :]

        B_const = Buf()
        B_ident = Buf()
        B_slot = [Buf() for _ in range(NSLOT)]
        B_bank = [Buf() for _ in range(8)]
        B_tails = Buf()
        B_httail = Buf()
        B_stat = Buf()
        B_stat_p = Buf()
        st = {"slab": 0, "ring": 0, "tile": -1, "tslab": 0}
        wb_tok = {}
        pre_q = []

        def load_slab(src_ap, view8, preload=False):
            key = repr(src_ap)
            if not preload and pre_q:
                s_, key_ = pre_q.pop(0)
                assert key_ == key, (key_, key)
                return s_
            i = st["slab"]
            st["slab"] += 1
            s = i % NSLOT
            ti = st["tile"]
            idx = st["tslab"]
            st["tslab"] += 1
            dst = slot8(s) if view8 else slot16(s)
            if src_ap.shape[1] != dst.shape[1]:
                dst = dst[:, 0:src_ap.shape[1], :]
            if ti < 0 or CACHE_POLICY == "none":
                mode = "direct"
            elif idx < SLABS_P12:
                mode = "wb" if ti == 0 else "cache"
            else:
                mode = "direct" if ti == 0 else ("wb" if ti == 1 else "cache")
            if mode == "cache":
                P.dma("pool", "w%d" % s, lambda e, s=s, idx=idx: e.dma_start(out=wring[:, s, :], in_=wcache[idx]),
                      wr=[B_slot[s]], extra=[wb_tok[idx]])
            else:
                P.dma("pool", "w%d" % s, lambda e, o=dst, i_=src_ap: e.dma_start(out=o, in_=i_),
                      wr=[B_slot[s]])
                if mode == "wb":
                    wb_tok[idx] = P.dma("sp", "wb%d" % s,
                                        lambda e, s=s, idx=idx: e.dma_start(out=wcache[idx], in_=wring[:, s, :]),
                                        rd=[B_slot[s]])
            if preload:
                pre_q.append((s, key))
            return s

        def preload_quad(wv, col0, nslab):
            assert not pre_q
            for sl in range(nslab):
                load_slab(wv[:, sl * 8:(sl + 1) * 8, col0: col0 + 512], True, preload=True)

        def alloc_bank(banks=range(8)):
            banks = list(banks)
            while True:
                b = st["ring"] % 8
                st["ring"] += 1
                if b in banks:
                    return b

        def mm(out_ap, lhsT, rhs, start, stop):
            return lambda e: e.matmul(out_ap, lhsT=lhsT, rhs=rhs, start=start, stop=stop)

        def proj_ws(wv, col0, nslab, rhs_of, rd_bufs, N, evac, tail_bank=None, filler=None, banks=range(8)):
            bb = [alloc_bank(banks) for _ in range(4)]
            for sl in range(nslab):
                s = load_slab(wv[:, sl * 8:(sl + 1) * 8, col0: col0 + 512], True)
                sv = slot8(s)
                fns = []
                for c in range(4):
                    for k in range(8):
                        fns.append(mm(bank(bb[c])[:, 0:N], sv[:, k, c * 128:(c + 1) * 128],
                                      rhs_of(sl * 8 + k), (sl == 0 and k == 0),
                                      (sl == nslab - 1 and k == 7)))
                wr_b = [B_bank[b] for b in bb]
                rd_b = [B_slot[s]] + list(rd_bufs)
                if tail_bank is not None:
                    for c in range(4):
                        for k in range(8):
                            fns.append(lambda e, c=c, k=k, sv=sv, sl=sl: e.matmul(
                                bank(tail_bank)[:, c * 16:(c + 1) * 16], lhsT=sv[:, k, c * 128:(c + 1) * 128],
                                rhs=httail[:, sl * 8 + k, :], start=False, stop=(sl == nslab - 1 and k == 7),
                                skip_group_check=True))
                    wr_b.append(B_bank[tail_bank])
                    rd_b.append(B_httail)
                if filler is None:
                    P.pe_group(fns, rd=rd_b, wr=wr_b)
                else:
                    assert tail_bank is None
                    P.pe_group(fns[:16], rd=rd_b, wr=wr_b[0:2])
                    filler()
                    P.pe_group(fns[16:], rd=rd_b, wr=wr_b[2:4])
                    filler()
            for c in range(4):
                evac(c, bb[c])

        def proj_wm(wv, col0, lhs_of, rd_bufs, evac, filler=None, banks=range(8)):
            bb = [alloc_bank(banks) for _ in range(4)]
            for sl in range(4):
                s = load_slab(wv[:, sl * 8:(sl + 1) * 8, col0:col0 + 512], True)
                sv = slot8(s)
                fns = []
                for tb in range(4):
                    for k in range(8):
                        kc = sl * 8 + k
                        fns.append(mm(bank(bb[tb]), lhs_of(kc, tb), sv[:, k, :], kc == 0, kc == 31))
                rd_b = [B_slot[s]] + list(rd_bufs)
                wr_b = [B_bank[b] for b in bb]
                if filler is None:
                    P.pe_group(fns, rd=rd_b, wr=wr_b)
                else:
                    P.pe_group(fns[:16], rd=rd_b, wr=wr_b[0:2])
                    filler()
                    P.pe_group(fns[16:], rd=rd_b, wr=wr_b[2:4])
                    filler()
            for tb in range(4):
                evac(tb, bb[tb])

        def setup():
            tmpI = regC[:, 0:128]
            tmpH = regC[:, 128:256]
            tmpM = regC[:, 256:512]
            tmpG = regC[:, 640:704]
            Bt = B_const
            for (o_, i_) in ((tmpI, ident_d), (tmpH, hv_d), (tmpM, emask), (tmpG, gb_fm), (psc[:, :], ps_fm),
                             (bfar[:, :], bfar_d), (invc[:, :], invc_d),
                             (etab[:, :, :], ebias.rearrange("h p c -> p h c"))):
                P.dma("sp", "cst", lambda e, o_=o_, i_=i_: e.dma_start(out=o_, in_=i_), wr=[Buf()])
            B_const.w = ("cst", 16 * P.dma_count["cst"])
            B_ident.w = P.run("act", lambda e: e.activation(out=ident_bf[:, :], in_=tmpI, func=AF.Copy), rd=[Bt], wr=[B_const])
            P.run("act", lambda e: e.activation(out=hv_bf[:, :], in_=tmpH, func=AF.Copy), rd=[Bt], wr=[B_const])
            P.run("dve", lambda e: e.memset(ones_bf[:, :], 1.0), wr=[B_const])
            P.run("dve", lambda e: e.memset(stat[:, 12:13], EPS), wr=[B_const])
            P.run("dve", lambda e: e.tensor_scalar(out=hb[:, :], in0=tmpG, scalar1=0.5, scalar2=None, op0=ALU.mult),
                  rd=[Bt], wr=[B_const])
            for h4 in range(4):
                P.run("act", lambda e, h4=h4: e.activation(out=etab[:, h4 * 4:(h4 + 1) * 4, :],
                                                           in_=etab[:, h4 * 4:(h4 + 1) * 4, :], func=AF.Exp),
                      wr=[B_const])
            for h in range(16):
                P.run("dve", lambda e, h=h: e.tensor_tensor(out=etab[:, h, :], in0=etab[:, h, :], in1=tmpM,
                                                           op=ALU.mult), rd=[Bt], wr=[B_const])
            return Bt

        eps_ap = stat[:, 12:13]
        B_setup_tmp = setup()
        state = {"regA_prev": {"hT": [], "ya": [], "yp": []}, "regC_prev": [B_setup_tmp], "R_prev_bufs": {0: [], 1: []}}

        def do_tile(ti):
            halo = ti < 0
            st["tile"] = ti
            st["tslab"] = 0
            kcur = (ti + 1) % 2
            kprev = 1 - kcur
            xsrc = x_halo if halo else x_own[ti * T:(ti + 1) * T, :]
            inhA = state["regA_prev"]
            inhC = state["regC_prev"]
            inhRcur = state["R_prev_bufs"][kcur]

            B_gain = Buf(inhA["ya"])
            B_xn = [Buf(inhA["yp"]), Buf(inhA["yp"])]
            hT_inh = list(Buf(inhA["hT"]).r)
            B_hTa, B_hTd = Buf(), Buf()
            B_hTl = [B_hTa, B_hTd]
            hT_last = {}
            B_xst = [Buf(inhRcur), Buf(inhRcur)]
            xpf = state.pop("xpf", None)
            if xpf is not None:
                B_xst = xpf
            P.dma("sp", "gn", lambda e: e.dma_start(out=gain_bc, in_=ng.partition_broadcast(128)), wr=[B_gain])
            xst = XST(kcur)
            use_junk = ti >= 1
            junkv = regC[:, 2048:4096].bitcast(BF16)
            B_junk = Buf(inhC)
            if use_junk:
                inhC = list(inhC) + [B_junk]
            def pro_stage1(tb):
                sl = tb % 2
                if not (xpf is not None and tb < 2):
                    P.dma("sp", "xs%d" % sl,
                          lambda e, sl=sl, tb=tb: e.dma_start(out=xst[:, sl, :], in_=xsrc[tb * 128:(tb + 1) * 128, :]),
                          wr=[B_xst[sl]])
                if use_junk:
                    P.run("act", lambda e, sl=sl, tb=tb: e.activation(out=junkv, in_=xst[:, sl, :], func=AF.Square,
                                                                     accum_out=stat[:, tb:tb + 1]),
                          rd=[B_xst[sl]], wr=[B_junk, B_stat_p])
                else:
                    P.run("act", lambda e, sl=sl, tb=tb: e.activation(out=xn[:, sl, :], in_=xst[:, sl, :],
                                                                     func=AF.Square, accum_out=stat[:, tb:tb + 1]),
                          rd=[B_xst[sl]], wr=[B_xn[sl], B_stat_p])
                P.run("act", lambda e, tb=tb: e.activation(out=stat[:, 4 + tb:5 + tb], in_=stat[:, tb:tb + 1],
                                                          func=AF.Sqrt, scale=1.0 / D, bias=eps_ap),
                      rd=[B_const], wr=[B_stat_p])
                P.run("dve", lambda e, tb=tb: e.reciprocal(out=stat[:, 8 + tb:9 + tb], in_=stat[:, 4 + tb:5 + tb]),
                      wr=[B_stat_p])
                P.run("dve", lambda e, sl=sl, tb=tb: e.scalar_tensor_tensor(
                    out=xn[:, sl, :], in0=xst[:, sl, :], scalar=stat[:, 8 + tb:9 + tb], in1=gain_bc,
                    op0=ALU.mult, op1=ALU.mult), rd=[B_xst[sl], B_gain, B_stat_p], wr=[B_xn[sl]])

            def pro_stage2(tb):
                sl = tb % 2
                for g8 in range(8):
                    b = alloc_bank()
                    bv = bank(b).bitcast(BF16)
                    fns = []
                    for i in range(4):
                        kc = g8 * 4 + i
                        fns.append(lambda e, bv=bv, i=i, kc=kc, sl=sl: e.transpose(
                            out=bv[:, i * 128:(i + 1) * 128], in_=xn[:, sl, kc * 128:(kc + 1) * 128],
                            identity=ident_bf[:, :]))
                    P.pe_group(fns, rd=[B_xn[sl], B_ident], wr=[B_bank[b]])
                    eng = "act" if g8 % 2 == 0 else "dve"
                    src = bv[:, 0:512].rearrange("p (a b) -> p a b", a=4)
                    dst = hT[:, g8 * 4:(g8 + 1) * 4, tb * 128:(tb + 1) * 128]
                    if eng == "act":
                        hT_last["act"] = P.run("act", lambda e, s_=src, d_=dst: e.activation(out=d_, in_=s_, func=AF.Copy),
                                               rd=[B_bank[b]], extra=hT_inh)
                    else:
                        hT_last["dve"] = P.run("dve", lambda e, s_=src, d_=dst: e.tensor_copy(out=d_, in_=s_),
                                               rd=[B_bank[b]], extra=hT_inh)


            pro_stage1(0)
            pro_stage1(1)
            pro_stage2(0)
            pro_stage1(2)
            pro_stage2(1)
            pro_stage1(3)
            pro_stage2(2)
            pro_stage2(3)
            B_hTa.w = hT_last["act"]
            B_hTd.w = hT_last["dve"]

            def rhs_h(N0, N1):
                return lambda kc: hT[:, kc, N0:N1]

            B_KT = [Buf(B_xst) for _ in range(16)]
            B_V = [Buf(B_xst) for _ in range(4)]
            KTc, Vc = KT(kcur), VV(kcur)

            def evac_k(G):
                def f(ci, b):
                    h = G * 4 + ci
                    P.run("act", lambda e: e.activation(out=KTc[:, h, :], in_=bank(b), func=AF.Copy),
                          rd=[B_bank[b]], wr=[B_KT[h]])
                return f

            def evac_v(G):
                def f(tb, b):
                    eng = "act" if tb % 2 == 0 else "dve"
                    dst = Vc[:, tb, G * 512:(G + 1) * 512]
                    if eng == "act":
                        P.run("act", lambda e: e.activation(out=dst, in_=bank(b), func=AF.Copy),
                              rd=[B_bank[b]], wr=[B_V[G]])
                    else:
                        P.run("dve", lambda e: e.tensor_copy(out=dst, in_=bank(b)), rd=[B_bank[b]], wr=[B_V[G]])
                return f

            if halo:
                P.run("act", lambda e: e.activation(out=httail[:, :, :], in_=hT[:, :, 496:512], func=AF.Copy),
                      rd=B_hTl, wr=[B_httail])
                for G in range(4):
                    proj_ws(w_in_v, C_K + G * 512, 4, rhs_h(0, 512), B_hTl, 512, evac_k(G))
                    proj_wm(w_in_v, C_V + G * 512, lambda kc, tb: hT[:, kc, tb * 128:(tb + 1) * 128], B_hTl,
                            evac_v(G))
                state["regA_prev"] = {"hT": B_hTl, "ya": [B_gain], "yp": [B_xn[0], B_xn[1]]}
                nxt0 = XST(1 - kcur)
                B_pf0 = [Buf(), Buf()]
                for tbn in range(2):
                    P.dma("sp", "xs%d" % tbn,
                          lambda e, tbn=tbn: e.dma_start(out=nxt0[:, tbn, :], in_=x_own[tbn * 128:(tbn + 1) * 128, :]),
                          wr=[B_pf0[tbn]])
                state["xpf"] = B_pf0
                state["regC_prev"] = inhC
                state["R_prev_bufs"][kcur] = []
                state["KV"] = (B_KT, B_V, kcur)
                return

            B_KTp, B_Vp, kp_chk = state["KV"]
            assert kp_chk == kprev
            KTp, Vp = KT(kprev), VV(kprev)

            Ub = [regC[:, 0:528], regC[:, 528:1056]]
            T1 = regC[:, 1056:1584]
            T2 = regC[:, 1584:2112]
            dT = regC[:, 2112:4160].bitcast(BF16).rearrange("p (g a b) -> p g a b", g=2, a=4)
            thz = regC[:, 4160:5184].rearrange("p (a b) -> p a b", a=2)
            B_U = [Buf(inhC), Buf(inhC)]
            B_T1, B_T2 = Buf(inhC), Buf(inhC)
            B_dT = [Buf(inhC), Buf(inhC)]
            B_thz = [Buf(inhC), Buf(inhC)]
            B_yp = [Buf([B_xn[0], B_xn[1]]) for _ in range(16)]
            B_ya = [Buf([B_gain]) for _ in range(16)]
            ucount = {"n": 0}

            def evac_u(g):
                def f(ci, b):
                    c = g * 4 + ci
                    w = POOL_W[g]
                    ub = ucount["n"] % 2
                    ucount["n"] += 1
                    U = Ub[ub]
                    P.run("act", lambda e: e.activation(out=U[:, 16:528], in_=bank(b), func=AF.Copy),
                          rd=[B_bank[b]], wr=[B_U[ub]])
                    P.run("act", lambda e: e.activation(out=U[:, 0:16], in_=tails[:, c, :], func=AF.Copy),
                          rd=[B_tails], wr=[B_U[ub]])
                    P.run("act", lambda e: e.activation(out=tails[:, c, :], in_=U[:, 512:528], func=AF.Copy),
                          rd=[B_U[ub]], wr=[B_tails])
                    src, Bsrc = U, B_U[ub]
                    steps = {2: [1], 4: [1, 2], 8: [1, 2, 4], 16: [1, 2, 4, 8]}[w]
                    lo = 0
                    tgt = [(T1, B_T1), (T2, B_T2)]
                    for si, sh in enumerate(steps):
                        dst, Bdst = tgt[si % 2]
                        lo2 = lo + sh
                        P.run("dve", lambda e, dst=dst, src=src, lo2=lo2, sh=sh: e.tensor_tensor(
                            out=dst[:, lo2:528], in0=src[:, lo2:528], in1=src[:, lo2 - sh:528 - sh], op=ALU.add),
                            rd=[Bsrc], wr=[Bdst])
                        src, Bsrc, lo = dst, Bdst, lo2
                    db = g % 2
                    P.run("dve", lambda e, src=src: e.scalar_tensor_tensor(
                        out=dT[:, db, ci, :], in0=src[:, 16:528], scalar=1.0 / w, in1=U[:, 16:528],
                        op0=ALU.mult, op1=ALU.subtract), rd=[Bsrc, B_U[ub]], wr=[B_dT[db]])
                    if ti == 0:
                        P.run("dve", lambda e, src=src: e.tensor_tensor(
                            out=T1[:, 0:16] if src is not T1 else T2[:, 0:16], in0=src[:, 16:32],
                            in1=invc[:, g * 16:(g + 1) * 16], op=ALU.mult),
                            rd=[Bsrc, B_const], wr=[B_T1 if src is not T1 else B_T2])
                        tmp = T1 if src is not T1 else T2
                        Btmp = B_T1 if src is not T1 else B_T2
                        P.run("dve", lambda e, tmp=tmp: e.tensor_tensor(
                            out=dT[:, db, ci, 0:16], in0=tmp[:, 0:16], in1=U[:, 16:32], op=ALU.subtract),
                            rd=[Btmp, B_U[ub]], wr=[B_dT[db]])
                return f

            def pool_group(g):
                db = g % 2
                zb = {}

                def evac_z(ci, b):
                    zb[ci] = b
                proj_ws(w_in_v, C_ZP + g * 512, 4, rhs_h(0, 512), B_hTl, 512, evac_z)
                s = load_slab(pool_w_v[:, g * 4:(g + 1) * 4, 0:512], True)
                sv = slot8(s)
                for ci in range(4):
                    c = g * 4 + ci
                    tz = ci % 2
                    pb = alloc_bank()
                    fns = [mm(bank(pb), sv[:, k, ci * 128:(ci + 1) * 128], dT[:, db, k, :], k == 0, k == 3)
                           for k in range(4)]
                    P.pe_group(fns, rd=[B_slot[s], B_dT[db]], wr=[B_bank[pb]])
                    z = zb[ci]
                    P.run("act", lambda e, z=z, tz=tz: e.activation(out=thz[:, tz, :], in_=bank(z), func=AF.Tanh,
                                                                   scale=0.5), rd=[B_bank[z]], wr=[B_thz[tz]])
                    P.run("dve", lambda e, z=z, tz=tz: e.scalar_tensor_tensor(
                        out=thz[:, tz, :], in0=thz[:, tz, :], scalar=1.0, in1=bank(z), op0=ALU.add, op1=ALU.mult),
                        rd=[B_bank[z]], wr=[B_thz[tz]])
                    P.run("dve", lambda e, pb=pb, tz=tz, c=c: e.scalar_tensor_tensor(
                        out=ypT[:, c, :], in0=bank(pb), scalar=psc[:, c:c + 1], in1=thz[:, tz, :],
                        op0=ALU.mult, op1=ALU.mult), rd=[B_bank[pb], B_thz[tz], B_const], wr=[B_yp[c]])

            def u_proj(g):
                if ti == 0:
                    tbk = alloc_bank()
                    P.run("dve", lambda e: e.memset(bank(tbk)[:, 0:64], 0.0), wr=[B_bank[tbk]])
                    ev = evac_u(g)

                    def ev2(ci, b):
                        if ci == 0:
                            P.run("act", lambda e: e.activation(
                                out=tails[:, g * 4:(g + 1) * 4, :],
                                in_=bank(tbk)[:, 0:64].rearrange("p (a b) -> p a b", a=4), func=AF.Copy),
                                rd=[B_bank[tbk]], wr=[B_tails])
                        ev(ci, b)
                    proj_ws(w_in_v, C_U + g * 512, 4, rhs_h(0, 512), B_hTl, 512, ev2, tail_bank=tbk)
                else:
                    proj_ws(w_in_v, C_U + g * 512, 4, rhs_h(0, 512), B_hTl, 512, evac_u(g))

            u_proj(0)
            for g in range(4):
                if g + 1 < 4:
                    u_proj(g + 1)
                pool_group(g)

            poolC = B_U + [B_T1, B_T2] + B_dT + B_thz

            qT2 = regC[:, 0:2048].bitcast(BF16).rearrange("p (g a b) -> p g a b", g=2, a=4)
            Xb = regC[:, 2048:2560].rearrange("p (a b) -> p a b", a=2)
            PT = regC[:, 2560:3200].bitcast(BF16).rearrange("p (a b) -> p a b", a=2)
            tha = regC[:, 3200:5248].rearrange("p (a b) -> p a b", a=4)
            rden = regC[:, 5248:5760]
            tt = rden
            B_qT2 = [[Buf(poolC) for _ in range(4)] for _ in range(2)]
            B_X = [Buf(poolC), Buf(poolC)]
            B_PT = [Buf(poolC), Buf(poolC)]
            B_sz = [Buf(poolC) for _ in range(4)]
            B_rq = [Buf(poolC) for _ in range(4)]
            ucnt = {"n": 0}

            def evac_q_of(G):
                def f(ci, b):
                    P.run("act", lambda e: e.activation(out=qT2[:, G % 2, ci, :], in_=bank(b), func=AF.Copy),
                          rd=[B_bank[b]], wr=[B_qT2[G % 2][ci]])
                return f

            def unit_S(h, hl, qb, sb=None):
                u = ucnt["n"] % 2
                ucnt["n"] += 1
                if sb is None:
                    a = alloc_bank(range(4))
                    bq = alloc_bank(range(4))
                    ee = "pool"
                else:
                    a, bq = sb
                    ee = "dve"
                qT = qT2[:, (h // 4) % 2, :, :]
                fns, rd = [], [B_qT2[(h // 4) % 2][hl]]
                for (bk, col, j) in ((a, 0, 0), (a, 1, 1), (a, 2, 2), (bq, 0, 3), (bq, 1, 4)):
                    r = qb + j
                    src, Bs = (KTp, B_KTp) if r < 4 else (KTc, B_KT)
                    kb = r % 4
                    fns.append(mm(bank(bk)[:, col * 128:(col + 1) * 128], src[:, h, kb * 128:(kb + 1) * 128],
                                  qT[:, hl, qb * 128:(qb + 1) * 128], True, True))
                    rd.append(Bs[h])
                P.pe_group(fns, rd=rd, wr=[B_bank[a], B_bank[bq]])
                P.run("act", lambda e: e.activation(out=PT[:, u, 0:384], in_=bank(a)[:, 0:384], func=AF.Exp,
                                                    scale=SCALE, bias=bfar[:, h:h + 1]),
                      rd=[B_bank[a], B_const], wr=[B_PT[u]])
                P.run(ee, lambda e: e.memset(PT[0:64, u, 64:128], 0.0), wr=[B_PT[u]])
                P.run("act", lambda e: e.activation(out=Xb[:, u, :], in_=bank(bq)[:, 0:256], func=AF.Exp, scale=SCALE),
                      rd=[B_bank[bq]], wr=[B_X[u]])
                P.run(ee, lambda e: e.tensor_tensor(out=PT[:, u, 384:640], in0=Xb[:, u, :], in1=etab[:, h, :],
                                                    op=ALU.mult), rd=[B_X[u], B_const], wr=[B_PT[u]])
                return u

            def unit_PV(h, qb, u, ob, db_):
                fns, rd = [], [B_PT[u], B_const]
                order = (0, 1, 2, 3, 4)
                for bi, j in enumerate(order):
                    r = qb + j
                    src, Bs = (Vp, B_Vp) if r < 4 else (Vc, B_V)
                    kb = r % 4
                    pt = PT[:, u, bi * 128:(bi + 1) * 128]
                    fns.append(mm(bank(ob)[:, qb * 128:(qb + 1) * 128], src[:, kb, h * 128:(h + 1) * 128], pt,
                                  bi == 0, bi == 4))
                    one = hv_bf if (ti == 0 and r < 4) else ones_bf
                    fns.append(mm(bank(db_)[:, qb * 128:(qb + 1) * 128], one[:, :], pt, bi == 0, bi == 4))
                    rd.append(Bs[h // 4])
                P.pe_group(fns, rd=rd, wr=[B_bank[ob], B_bank[db_]])

            def head_evac(h, ci, ob, db_, ee="pool"):
                P.run("dve", lambda e: e.reciprocal(out=rden, in_=bank(db_)), rd=[B_bank[db_]], wr=B_rq)
                P.run(ee, lambda e: e.tensor_tensor(out=rden, in0=rden, in1=tha[:, ci, :], op=ALU.mult),
                      rd=[B_sz[ci]], wr=B_rq)
                P.run("dve", lambda e: e.tensor_tensor(out=yaT[:, h, :], in0=bank(ob), in1=rden, op=ALU.mult),
                      rd=[B_bank[ob]] + B_rq, wr=[B_ya[h]])

            def unit_evac(h, ci, qb, ob, db_):
                cs = slice(qb * 128, (qb + 1) * 128)
                P.run("dve", lambda e: e.reciprocal(out=rden[:, cs], in_=bank(db_)[:, cs]),
                      rd=[B_bank[db_]], wr=[B_rq[qb]])
                P.run("dve", lambda e: e.tensor_tensor(out=rden[:, cs], in0=rden[:, cs], in1=tha[:, ci, cs],
                                                       op=ALU.mult), rd=[B_sz[ci]], wr=[B_rq[qb]])
                P.run("dve", lambda e: e.tensor_tensor(out=yaT[:, h, cs], in0=bank(ob)[:, cs], in1=rden[:, cs],
                                                       op=ALU.mult), rd=[B_bank[ob], B_rq[qb]], wr=[B_ya[h]])

            def A_qkv(G, filler, banks):
                proj_ws(w_in_v, C_Q + G * 512, 4, rhs_h(0, 512), B_hTl, 512, evac_q_of(G), filler=filler, banks=banks)
                proj_ws(w_in_v, C_K + G * 512, 4, rhs_h(0, 512), B_hTl, 512, evac_k(G), filler=filler, banks=banks)
                proj_wm(w_in_v, C_V + G * 512, lambda kc, tb: hT[:, kc, tb * 128:(tb + 1) * 128], B_hTl, evac_v(G),
                        filler=filler, banks=banks)

            def Zq(G):
                zb = {}

                def evac_z(ci, b):
                    zb[ci] = b
                proj_ws(w_in_v, C_ZA + G * 512, 4, rhs_h(0, 512), B_hTl, 512, evac_z)
                for ci in range(4):
                    z = zb[ci]
                    P.run("act", lambda e, z=z, ci=ci: e.activation(out=tha[:, ci, :], in_=bank(z), func=AF.Tanh,
                                                                   scale=0.5), rd=[B_bank[z]], wr=[B_sz[ci]])
                    P.run("dve", lambda e, z=z, ci=ci: e.scalar_tensor_tensor(
                        out=tha[:, ci, :], in0=tha[:, ci, :], scalar=1.0, in1=bank(z), op0=ALU.add, op1=ALU.mult),
                        rd=[B_bank[z]], wr=[B_sz[ci]])

            def unit_steps(G):
                h0 = G * 4
                units = [(ci, qb) for ci in range(4) for qb in range(4)]
                prev = None
                for nxt in units + [None]:
                    if prev is not None:
                        pci, pqb, pu = prev
                        unit_PV(h0 + pci, pqb, pu, 6, 7)
                        unit_evac(h0 + pci, pci, pqb, 6, 7)
                    if nxt is not None:
                        ci, qb = nxt
                        u = unit_S(h0 + ci, ci, qb, sb=(4, 5))
                        prev = (ci, qb, u)
                    yield 0

            A_qkv(0, None, range(8))
            Zq(0)
            for G in range(4):
                if G < 3:
                    gen = unit_steps(G)
                    A_qkv(G + 1, lambda gen=gen: next(gen, None), range(4))
                    for _ in gen:
                        pass
                    Zq(G + 1)
                else:
                    preload_quad(w_in_v, C_GA, 4)
                    units = [(ci, qb) for ci in range(4) for qb in range(4)]
                    obank = {0: (4, 5), 1: (6, 7)}
                    us = {}
                    h0 = G * 4
                    us[0] = unit_S(h0 + units[0][0], units[0][0], units[0][1])
                    for i, (ci, qb) in enumerate(units):
                        if i + 1 < len(units):
                            ci2, qb2 = units[i + 1]
                            us[i + 1] = unit_S(h0 + ci2, ci2, qb2)
                        ob, db_ = 4 + (qb % 2), 6 + (qb % 2)
                        unit_PV(h0 + ci, qb, us[i], ob, db_)
                        unit_evac(h0 + ci, ci, qb, ob, db_)

            attnC = B_qT2[0] + B_qT2[1] + B_X + B_PT + B_sz + B_rq
            assert st["tslab"] == SLABS_P12 + len(pre_q), st["tslab"]

            mT = MT(kprev)
            inh_m = list(B_KTp) + list(B_Vp)
            B_m = [Buf(inh_m) for _ in range(32)]
            th3 = regC[:, 0:4096].rearrange("p (x a b) -> p x a b", x=2, a=4)
            B_th3 = [[Buf(attnC) for _ in range(4)], [Buf(attnC) for _ in range(4)]]
            for jq in range(8):
                for br in range(2):
                    gcol = (C_GA if br == 0 else C_GP) + jq * 512
                    wo_v = w_oa_v if br == 0 else w_op_v
                    yT = yaT if br == 0 else ypT
                    B_y = B_ya if br == 0 else B_yp
                    gbk = {}

                    def evac_g(ci, b):
                        gbk[ci] = b
                    proj_ws(w_in_v, gcol, 4, rhs_h(0, 512), B_hTl, 512, evac_g)
                    obk = {}

                    def evac_o(ci, b):
                        obk[ci] = b
                    proj_ws(wo_v, jq * 512, 2, lambda kc, yT=yT: yT[:, kc, :], B_y, 512, evac_o)
                    for ci in range(4):
                        c = jq * 4 + ci
                        gb_, ob_ = gbk[ci], obk[ci]
                        P.run("act", lambda e, gb_=gb_, ci=ci, c=c, br=br: e.activation(
                            out=th3[:, br, ci, :], in_=bank(gb_), func=AF.Tanh, scale=0.5,
                            bias=hb[:, br * 32 + c: br * 32 + c + 1]),
                            rd=[B_bank[gb_], B_const], wr=[B_th3[br][ci]])
                        P.run("dve", lambda e, ob_=ob_, ci=ci, br=br: e.scalar_tensor_tensor(
                            out=th3[:, br, ci, :], in0=th3[:, br, ci, :], scalar=1.0, in1=bank(ob_),
                            op0=ALU.add, op1=ALU.mult), rd=[B_bank[ob_]], wr=[B_th3[br][ci]])
                        if br == 1:
                            P.run("dve", lambda e, ci=ci, c=c: e.tensor_tensor(
                                out=mT[:, c, :], in0=th3[:, 0, ci, :], in1=th3[:, 1, ci, :], op=ALU.add),
                                rd=[B_th3[0][ci], B_th3[1][ci]], wr=[B_m[c]])

            ph3C = B_th3[0] + B_th3[1]

            xs4 = regC[:, 0:4096].rearrange("p (s a b) -> p s a b", s=2, a=4)
            fgE = regC[:, 4096:5120].rearrange("p (a b) -> p a b", a=2)
            B_xs4 = [Buf(ph3C), Buf(ph3C)]
            B_fgE = [Buf(ph3C), Buf(ph3C)]
            inh_o = B_hTl + B_ya + B_yp
            B_ob = [Buf(inh_o) for _ in range(4)]
            xo = x_own[ti * T:(ti + 1) * T, :].rearrange("(a p) c -> p a c", p=128)
            oo = out[ti * T:(ti + 1) * T, :].rearrange("(a p) c -> p a c", p=128)
            for eb in range(8):
                sl = eb % 2
                P.dma("sp", "x4%d" % sl,
                      lambda e, sl=sl, eb=eb: e.dma_start(out=xs4[:, sl, :, :], in_=xo[:, :, eb * 512:(eb + 1) * 512]),
                      wr=[B_xs4[sl]])
                P.dma("sp", "fg%d" % sl,
                      lambda e, sl=sl, eb=eb: e.dma_start(out=fgE[:, sl, :],
                                                          in_=fg[eb * 512:(eb + 1) * 512].partition_broadcast(128)),
                      wr=[B_fgE[sl]])

                def evac_o4(tb, b, eb=eb, sl=sl):
                    dst = outb[:, tb, eb * 512:(eb + 1) * 512]
                    P.run("dve", lambda e: e.scalar_tensor_tensor(out=dst, in0=bank(b), scalar=0.25,
                                                                 in1=xs4[:, sl, tb, :], op0=ALU.mult, op1=ALU.add),
                          rd=[B_bank[b], B_xs4[sl]], wr=[B_ob[tb]])
                    P.run("act", lambda e: e.activation(out=bank(b), in_=dst, func=AF.Square,
                                                        accum_out=stat[:, 32 + tb * 8 + eb: 33 + tb * 8 + eb]),
                          rd=[B_ob[tb]], wr=[B_bank[b], B_stat])
                    P.run("dve", lambda e: e.tensor_tensor(out=dst, in0=dst, in1=fgE[:, sl, :], op=ALU.mult),
                          rd=[B_fgE[sl]], wr=[B_ob[tb]])
                proj_wm(w_out_v, eb * 512, lambda kc, tb: mT[:, kc, tb * 128:(tb + 1) * 128], B_m, evac_o4)
            if ti + 1 < NTILE:
                nxt = XST(kprev)
                xs_n = x_own[(ti + 1) * T:(ti + 2) * T, :]
                B_pf = [Buf(B_m), Buf(B_m)]
                for tbn in range(2):
                    P.dma("sp", "xs%d" % tbn,
                          lambda e, tbn=tbn: e.dma_start(out=nxt[:, tbn, :], in_=xs_n[tbn * 128:(tbn + 1) * 128, :]),
                          wr=[B_pf[tbn]])
                state["xpf"] = B_pf
            for tb in (2, 3, 0, 1):
                P.run("dve", lambda e, tb=tb: e.tensor_reduce(out=stat[:, 16 + tb:17 + tb],
                                                             in_=stat[:, 32 + tb * 8: 40 + tb * 8],
                                                             axis=mybir.AxisListType.X, op=ALU.add),
                      wr=[B_stat])
                P.run("act", lambda e, tb=tb: e.activation(out=stat[:, 20 + tb:21 + tb], in_=stat[:, 16 + tb:17 + tb],
                                                          func=AF.Sqrt, scale=1.0 / D, bias=eps_ap),
                      rd=[B_const], wr=[B_stat])
                P.run("dve", lambda e, tb=tb: e.reciprocal(out=stat[:, 24 + tb:25 + tb], in_=stat[:, 20 + tb:21 + tb]),
                      wr=[B_stat])
            for tb in (2, 3, 0, 1):
                if tb in (2, 0):
                    P.run("dve", lambda e, tb=tb: e.tensor_scalar(
                        out=outb[:, tb, :], in0=outb[:, tb, :], scalar1=stat[:, 24 + tb:25 + tb], scalar2=None,
                        op0=ALU.mult), rd=[B_stat], wr=[B_ob[tb]])
                else:
                    P.run("act", lambda e, tb=tb: e.activation(
                        out=outb[:, tb, :], in_=outb[:, tb, :], func=AF.Copy, scale=stat[:, 24 + tb:25 + tb]),
                        rd=[B_stat], wr=[B_ob[tb]])
                P.dma("sp", "out%d" % tb, lambda e, tb=tb: e.dma_start(out=oo[:, tb, :], in_=outb[:, tb, :]),
                      rd=[B_ob[tb]])

            assert st["tslab"] == SLABS_TILE, st["tslab"]
            state["regA_prev"] = {"hT": [B_ob[0], B_ob[1]], "ya": [B_ob[2]], "yp": [B_ob[3]]}
            state["regC_prev"] = B_xs4 + B_fgE
            state["R_prev_bufs"][kprev] = list(B_m)
            state["R_prev_bufs"][kcur] = []
            state["KV"] = (B_KT, B_V, kcur)
            state["last_ob"] = B_ob

        do_tile(-1)
        for ti in range(NTILE):
            do_tile(ti)
        for tb in range(4):
            P.s["sp"].wait(("out%d" % tb, 16 * P.dma_count["out%d" % tb]))

        def replay(stream, eng):
            for op in stream.ops:
                if op[0] == "wait":
                    eng.wait_ge(sems[op[1]], op[2])
                else:
                    ins = op[1](eng)
                    if op[2] is not None:
                        ins.then_inc(sems[op[2]], op[3])

        @block.tensor
        def _(e):
            replay(P.s["pe"], e)

        @block.scalar
        def _(e):
            replay(P.s["act"], e)

        @block.vector
        def _(e):
            replay(P.s["dve"], e)

        @block.gpsimd
        def _(e):
            replay(P.s["pool"], e)

        @block.sync
        def _(e):
            replay(P.s["sp"], e)

    return nc


_NC_CACHE = {}


def _host_consts(rel_bias):
    ki = np.arange(128)[:, None]
    qi = np.arange(128)[None, :]
    idx3 = np.minimum(qi - ki, 0) + 256
    idx4 = qi - ki + 128
    eb = np.empty((16, 128, 256), np.float32)
    eb[:, :, 0:128] = rel_bias[:, idx3]
    eb[:, :, 128:256] = rel_bias[:, idx4]
    em = np.ones((128, 256), np.float32)
    em[64:128, 128:192] = 0.0
    bfar = np.ascontiguousarray(np.broadcast_to(rel_bias[:, 256][None, :], (128, 16))).astype(np.float32)
    return eb, em, bfar


def kernel(x, norm_gain, w_in, rel_bias, pool_w, pool_scale, w_out_attn, w_out_pool, gate_bias, w_out,
           final_gain):
    x = np.asarray(x, np.float32)
    f = lambda a: np.ascontiguousarray(np.asarray(a, np.float32))
    w_in, w_oa, w_op, w_o = f(w_in), f(w_out_attn), f(w_out_pool), f(w_out)
    pw = f(pool_w).reshape(2048, 512)
    ng, fgn = f(norm_gain), f(final_gain)
    gb = f(gate_bias)
    gb_fm = np.ascontiguousarray(gb.reshape(2, 32, 128).transpose(2, 0, 1).reshape(128, 64))
    ps_fm = np.ascontiguousarray(f(pool_scale).reshape(16, 128).T)
    eb, em, bfar = _host_consts(f(rel_bias))
    ident = np.eye(128, dtype=np.float32)

    if "nc" not in _NC_CACHE:
        _NC_CACHE["nc"] = build_program()
    nc = _NC_CACHE["nc"]

    in_maps = []
    for c in range(8):
        b, half = c // 2, c % 2
        own = np.ascontiguousarray(x[b, half * OWN:(half + 1) * OWN, :])
        if half == 0:
            halo = np.zeros((T, D), np.float32)
            hv = np.zeros((128, 128), np.float32)
            invc = np.empty((128, 64), np.float32)
            for g, w in enumerate(POOL_W):
                invc[:, g * 16:(g + 1) * 16] = (1.0 / np.minimum(np.arange(16) + 1, w)).astype(np.float32)[None, :]
        else:
            halo = np.ascontiguousarray(x[b, OWN - T:OWN, :])
            hv = np.ones((128, 128), np.float32)
            invc = np.empty((128, 64), np.float32)
            for g, w in enumerate(POOL_W):
                invc[:, g * 16:(g + 1) * 16] = np.float32(1.0 / w)
        in_maps.append({
            "x_own": own, "x_halo": halo, "w_in": w_in, "pool_w": pw, "w_oa": w_oa, "w_op": w_op, "w_out": w_o,
            "ng": ng, "fg": fgn, "gb_fm": gb_fm, "ps_fm": ps_fm, "ebias": eb, "emask": em, "bfar": bfar,
            "hv": hv, "invc": invc, "ident": ident,
        })
    res = run_bass_kernel_spmd(nc, in_maps, core_ids=list(range(8)))
    outp = np.empty((4, 4096, 4096), np.float32)
    for c in range(8):
        b, half = c // 2, c % 2
        outp[b, half * OWN:(half + 1) * OWN, :] = res.results[c]["out"]
    return outp
```

```python
import numpy as np
import concourse.bass as bass
import concourse.mybir as mybir
from concourse.bass_utils import run_bass_kernel_spmd

F32 = mybir.dt.float32
BF16 = mybir.dt.bfloat16
AF = mybir.ActivationFunctionType
ALU = mybir.AluOpType

D = 4096
T = 512
NTILE = 4
OWN = T * NTILE
AW = 2048
IN_COLS = 20480
C_Q, C_K, C_V, C_ZA, C_U, C_ZP, C_GA, C_GP = 0, 2048, 4096, 6144, 8192, 10240, 12288, 16384
EPS = 1e-6
SCALE = 128 ** -0.5
POOL_W = (2, 4, 8, 16)
NSLOT = 4
CACHE_POLICY = "none"
SLABS_P12 = 100
SLABS_TILE = 228


class Buf:
    __slots__ = ("w", "r")

    def __init__(self, inherit=()):
        self.w = None
        self.r = []
        for b in inherit:
            if b.w is not None:
                self.r.append(b.w)
            self.r.extend(b.r)
        self._prune()

    def _prune(self):
        best = {}
        for k, v in self.r:
            if best.get(k, 0) < v:
                best[k] = v
        self.r = list(best.items())


class Stream:
    def __init__(self, name):
        self.name = name
        self.ops = []
        self.count = 0
        self.waited = {}

    def wait(self, tok):
        if tok is None:
            return
        k, v = tok
        if self.name == "pe" and k == "pe":
            return
        if self.waited.get(k, 0) >= v:
            return
        self.waited[k] = v
        self.ops.append(("wait", k, v))

    def emit(self, fn, signal):
        if signal:
            self.count += 1
            self.ops.append(("op", fn, self.name, 1))
            return (self.name, self.count)
        self.ops.append(("op", fn, None, 0))
        return None


class Prog:
    def __init__(self):
        self.s = {n: Stream(n) for n in ("pe", "act", "dve", "pool", "sp")}
        self.dma_count = {}

    def _deps(self, rd, wr, extra):
        deps = list(extra)
        for b in rd:
            deps.append(b.w)
        for b in wr:
            deps.append(b.w)
            deps.extend(b.r)
        return deps

    def _mark(self, tok, rd, wr):
        for b in rd:
            b.r.append(tok)
            if len(b.r) > 12:
                b._prune()
        for b in wr:
            b.w = tok
            b.r = []

    def run(self, eng, fn, rd=(), wr=(), extra=()):
        st = self.s[eng]
        for d in self._deps(rd, wr, extra):
            st.wait(d)
        tok = st.emit(fn, True)
        self._mark(tok, rd, wr)
        return tok

    def pe_group(self, fns, rd=(), wr=(), extra=()):
        st = self.s["pe"]
        for d in self._deps(rd, wr, extra):
            st.wait(d)
        tok = None
        for i, fn in enumerate(fns):
            tok = st.emit(fn, i == len(fns) - 1)
        self._mark(tok, rd, wr)
        return tok

    def dma(self, eng, semkey, fn, rd=(), wr=(), extra=()):
        st = self.s[eng]
        for d in self._deps(rd, wr, extra):
            st.wait(d)
        self.dma_count[semkey] = self.dma_count.get(semkey, 0) + 1
        st.ops.append(("op", fn, semkey, 16))
        tok = (semkey, 16 * self.dma_count[semkey])
        self._mark(tok, rd, wr)
        return tok


def build_program():
    nc = bass.Bass("TRN2", target_bir_lowering=False)
    P = Prog()

    def din(name, shape):
        return nc.dram_tensor(name, list(shape), F32, kind="ExternalInput").ap()

    x_own = din("x_own", [OWN, D])
    x_halo = din("x_halo", [T, D])
    w_in = din("w_in", [D, IN_COLS])
    pool_w = din("pool_w", [2048, 512])
    w_oa = din("w_oa", [AW, D])
    w_op = din("w_op", [AW, D])
    w_out = din("w_out", [D, D])
    ng = din("ng", [D])
    fg = din("fg", [D])
    gb_fm = din("gb_fm", [128, 64])
    ps_fm = din("ps_fm", [128, 16])
    ebias = din("ebias", [16, 128, 256])
    emask = din("emask", [128, 256])
    bfar_d = din("bfar", [128, 16])
    hv_d = din("hv", [128, 128])
    invc_d = din("invc", [128, 64])
    ident_d = din("ident", [128, 128])
    out = nc.dram_tensor("out", [OWN, D], F32, kind="ExternalOutput").ap()
    wcache = nc.dram_tensor("wcache", [SLABS_TILE, 128, 4096], BF16, kind="Internal").ap()

    w_in_v = w_in.rearrange("(kc p) n -> p kc n", p=128)
    w_oa_v = w_oa.rearrange("(kc p) n -> p kc n", p=128)
    w_op_v = w_op.rearrange("(kc p) n -> p kc n", p=128)
    w_out_v = w_out.rearrange("(kc p) n -> p kc n", p=128)
    pool_w_v = pool_w.rearrange("(kc p) n -> p kc n", p=128)

    ctx = (
        nc.sbuf_tensor("sb_regA", [128, 16384], F32),
        nc.sbuf_tensor("sb_regR0", [128, 8192], F32),
        nc.sbuf_tensor("sb_regR1", [128, 8192], F32),
        nc.sbuf_tensor("sb_regC", [128, 6144], F32),
        nc.sbuf_tensor("sb_wring", [128, NSLOT, 4096], BF16),
        nc.sbuf_tensor("sb_etab", [128, 16, 256], F32),
        nc.sbuf_tensor("sb_ident_bf", [128, 128], BF16),
        nc.sbuf_tensor("sb_ones_bf", [128, 128], BF16),
        nc.sbuf_tensor("sb_hv_bf", [128, 128], BF16),
        nc.sbuf_tensor("sb_hb", [128, 64], F32),
        nc.sbuf_tensor("sb_psc", [128, 16], F32),
        nc.sbuf_tensor("sb_bfar", [128, 16], F32),
        nc.sbuf_tensor("sb_invc", [128, 64], F32),
        nc.sbuf_tensor("sb_tails", [128, 16, 16], F32),
        nc.sbuf_tensor("sb_stat", [128, 96], F32),
        nc.sbuf_tensor("sb_httail", [128, 32, 16], BF16),
        nc.psum_tensor("ps", [128, 8, 512], F32),
    )
    sem_names = ["pe", "act", "dve", "pool", "sp", "cst", "xs0", "xs1", "gn", "x40", "x41",
                 "fg0", "fg1", "fg2", "fg3", "out0", "out1", "out2", "out3"] + ["w%d" % i for i in range(NSLOT)] + ["wb%d" % i for i in range(NSLOT)]
    sem_ctx = [nc.semaphore("s_" + n) for n in sem_names]

    from contextlib import ExitStack
    with ExitStack() as es:
        (regA, regR0, regR1, regC, wring, etab, ident_bf, ones_bf, hv_bf, hb, psc, bfar, invc,
         tails, stat, httail, ps) = [es.enter_context(c) for c in ctx]
        sems = {n: es.enter_context(c) for n, c in zip(sem_names, sem_ctx)}
        block = es.enter_context(nc.Block())
        regR = [regR0, regR1]

        hT = regA[:, 0:8192].bitcast(BF16).rearrange("p (a b) -> p a b", a=32)
        yaT = regA[:, 8192:12288].bitcast(BF16).rearrange("p (a b) -> p a b", a=16)
        ypT = regA[:, 12288:16384].bitcast(BF16).rearrange("p (a b) -> p a b", a=16)
        outb = regA[:, :].rearrange("p (a b) -> p a b", a=4)
        gain_bc = regA[:, 8192:12288]
        xn = regA[:, 12288:16384].bitcast(BF16).rearrange("p (a b) -> p a b", a=2)

        def KT(k):
            return regR[k][:, 0:4096].bitcast(BF16).rearrange("p (a b) -> p a b", a=16)

        def VV(k):
            return regR[k][:, 4096:8192].bitcast(BF16).rearrange("p (a b) -> p a b", a=4)

        def MT(k):
            return regR[k][:, :].bitcast(BF16).rearrange("p (a b) -> p a b", a=32)

        def XST(k):
            return regR[k][:, :].rearrange("p (a b) -> p a b", a=2)

        def slot16(s):
            return wring[:, s, :].rearrange("p (a b) -> p a b", a=16)

        def slot8(s):
            return wring[:, s, :].rearrange("p (a b) -> p a b", a=8)

        def bank(b):
            return ps[:, b, :]

        B_const = Buf()
        B_ident = Buf()
        B_slot = [Buf() for _ in range(NSLOT)]
        B_bank = [Buf() for _ in range(8)]
        B_tails = Buf()
        B_httail = Buf()
        B_stat = Buf()
        B_stat_p = Buf()
        st = {"slab": 0, "ring": 0, "tile": -1, "tslab": 0}
        wb_tok = {}
        pre_q = []

        def load_slab(src_ap, view8, preload=False):
            key = repr(src_ap)
            if not preload and pre_q:
                s_, key_ = pre_q.pop(0)
                assert key_ == key, (key_, key)
                return s_
            i = st["slab"]
            st["slab"] += 1
            s = i % NSLOT
            ti = st["tile"]
            idx = st["tslab"]
            st["tslab"] += 1
            dst = slot8(s) if view8 else slot16(s)
            if src_ap.shape[1] != dst.shape[1]:
                dst = dst[:, 0:src_ap.shape[1], :]
            if ti < 0 or CACHE_POLICY == "none":
                mode = "direct"
            elif idx < SLABS_P12:
                mode = "wb" if ti == 0 else "cache"
            else:
                mode = "direct" if ti == 0 else ("wb" if ti == 1 else "cache")
            if mode == "cache":
                P.dma("pool", "w%d" % s, lambda e, s=s, idx=idx: e.dma_start(out=wring[:, s, :], in_=wcache[idx]),
                      wr=[B_slot[s]], extra=[wb_tok[idx]])
            else:
                crit = [("gn", 16), ("xs0", 16), ("xs1", 16)] if i < NSLOT else []
                P.dma("pool", "w%d" % s, lambda e, o=dst, i_=src_ap: e.dma_start(out=o, in_=i_),
                      wr=[B_slot[s]], extra=crit)
                if mode == "wb":
                    wb_tok[idx] = P.dma("sp", "wb%d" % s,
                                        lambda e, s=s, idx=idx: e.dma_start(out=wcache[idx], in_=wring[:, s, :]),
                                        rd=[B_slot[s]])
            if preload:
                pre_q.append((s, key))
            return s

        def preload_quad(wv, col0, nslab):
            assert not pre_q
            for sl in range(nslab):
                load_slab(wv[:, sl * 8:(sl + 1) * 8, col0: col0 + 512], True, preload=True)

        def alloc_bank(banks=range(8)):
            banks = list(banks)
            while True:
                b = st["ring"] % 8
                st["ring"] += 1
                if b in banks:
                    return b

        def mm(out_ap, lhsT, rhs, start, stop):
            return lambda e: e.matmul(out_ap, lhsT=lhsT, rhs=rhs, start=start, stop=stop)

        def proj_ws(wv, col0, nslab, rhs_of, rd_bufs, N, evac, tail_bank=None, filler=None, banks=range(8)):
            bb = [alloc_bank(banks) for _ in range(4)]
            for sl in range(nslab):
                s = load_slab(wv[:, sl * 8:(sl + 1) * 8, col0: col0 + 512], True)
                sv = slot8(s)
                fns = []
                for c in range(4):
                    for k in range(8):
                        fns.append(mm(bank(bb[c])[:, 0:N], sv[:, k, c * 128:(c + 1) * 128],
                                      rhs_of(sl * 8 + k), (sl == 0 and k == 0),
                                      (sl == nslab - 1 and k == 7)))
                wr_b = [B_bank[b] for b in bb]
                rd_b = [B_slot[s]] + list(rd_bufs)
                if tail_bank is not None:
                    for c in range(4):
                        for k in range(8):
                            fns.append(lambda e, c=c, k=k, sv=sv, sl=sl: e.matmul(
                                bank(tail_bank)[:, c * 16:(c + 1) * 16], lhsT=sv[:, k, c * 128:(c + 1) * 128],
                                rhs=httail[:, sl * 8 + k, :], start=False, stop=(sl == nslab - 1 and k == 7),
                                skip_group_check=True))
                    wr_b.append(B_bank[tail_bank])
                    rd_b.append(B_httail)
                if filler is None:
                    P.pe_group(fns, rd=rd_b, wr=wr_b)
                else:
                    assert tail_bank is None
                    P.pe_group(fns[:16], rd=rd_b, wr=wr_b[0:2])
                    filler()
                    P.pe_group(fns[16:], rd=rd_b, wr=wr_b[2:4])
                    filler()
            for c in range(4):
                evac(c, bb[c])

        def proj_wm(wv, col0, lhs_of, rd_bufs, evac, filler=None, banks=range(8)):
            bb = [alloc_bank(banks) for _ in range(4)]
            for sl in range(4):
                s = load_slab(wv[:, sl * 8:(sl + 1) * 8, col0:col0 + 512], True)
                sv = slot8(s)
                fns = []
                for tb in range(4):
                    for k in range(8):
                        kc = sl * 8 + k
                        fns.append(mm(bank(bb[tb]), lhs_of(kc, tb), sv[:, k, :], kc == 0, kc == 31))
                rd_b = [B_slot[s]] + list(rd_bufs)
                wr_b = [B_bank[b] for b in bb]
                if filler is None:
                    P.pe_group(fns, rd=rd_b, wr=wr_b)
                else:
                    P.pe_group(fns[:16], rd=rd_b, wr=wr_b[0:2])
                    filler()
                    P.pe_group(fns[16:], rd=rd_b, wr=wr_b[2:4])
                    filler()
            for tb in range(4):
                evac(tb, bb[tb])

        def setup():
            tmpI = regC[:, 0:128]
            tmpH = regC[:, 128:256]
            tmpM = regC[:, 256:512]
            tmpG = regC[:, 640:704]
            Bt = B_const
            for (o_, i_) in ((tmpI, ident_d), (tmpH, hv_d), (tmpM, emask), (tmpG, gb_fm), (psc[:, :], ps_fm),
                             (bfar[:, :], bfar_d), (invc[:, :], invc_d),
                             (etab[:, :, :], ebias.rearrange("h p c -> p h c"))):
                P.dma("sp", "cst", lambda e, o_=o_, i_=i_: e.dma_start(out=o_, in_=i_), wr=[Buf()])
            B_const.w = ("cst", 16 * P.dma_count["cst"])
            B_ident.w = P.run("act", lambda e: e.activation(out=ident_bf[:, :], in_=tmpI, func=AF.Copy), rd=[Bt], wr=[B_const])
            P.run("act", lambda e: e.activation(out=hv_bf[:, :], in_=tmpH, func=AF.Copy), rd=[Bt], wr=[B_const])
            P.run("dve", lambda e: e.memset(ones_bf[:, :], 1.0), wr=[B_const])
            P.run("dve", lambda e: e.memset(stat[:, 12:13], EPS), wr=[B_const])
            P.run("dve", lambda e: e.tensor_scalar(out=hb[:, :], in0=tmpG, scalar1=0.5, scalar2=None, op0=ALU.mult),
                  rd=[Bt], wr=[B_const])
            for h4 in range(4):
                P.run("act", lambda e, h4=h4: e.activation(out=etab[:, h4 * 4:(h4 + 1) * 4, :],
                                                           in_=etab[:, h4 * 4:(h4 + 1) * 4, :], func=AF.Exp),
                      wr=[B_const])
            for h in range(16):
                P.run("dve", lambda e, h=h: e.tensor_tensor(out=etab[:, h, :], in0=etab[:, h, :], in1=tmpM,
                                                           op=ALU.mult), rd=[Bt], wr=[B_const])
            return Bt

        eps_ap = stat[:, 12:13]
        B_setup_tmp = setup()
        state = {"regA_prev": {"hT": [], "ya": [], "yp": []}, "regC_prev": [B_setup_tmp], "R_prev_bufs": {0: [], 1: []}}

        def do_tile(ti):
            halo = ti < 0
            st["tile"] = ti
            st["tslab"] = 0
            kcur = (ti + 1) % 2
            kprev = 1 - kcur
            xsrc = x_halo if halo else x_own[ti * T:(ti + 1) * T, :]
            inhA = state["regA_prev"]
            inhC = state["regC_prev"]
            inhRcur = state["R_prev_bufs"][kcur]

            B_gain = Buf(inhA["ya"])
            B_xn = [Buf(inhA["yp"]), Buf(inhA["yp"])]
            hT_inh = list(Buf(inhA["hT"]).r)
            B_hTa, B_hTd = Buf(), Buf()
            B_hTl = [B_hTa, B_hTd]
            hT_last = {}
            B_xst = [Buf(inhRcur), Buf(inhRcur)]
            xpf = state.pop("xpf", None)
            if xpf is not None:
                B_xst = xpf
            P.dma("sp", "gn", lambda e: e.dma_start(out=gain_bc, in_=ng.partition_broadcast(128)), wr=[B_gain])
            xst = XST(kcur)
            use_junk = ti >= 1
            junkv = regC[:, 2048:4096].bitcast(BF16)
            B_junk = Buf(inhC)
            if use_junk:
                inhC = list(inhC) + [B_junk]
            def pro_stage1(tb):
                sl = tb % 2
                if not (xpf is not None and tb < 2):
                    P.dma("sp", "xs%d" % sl,
                          lambda e, sl=sl, tb=tb: e.dma_start(out=xst[:, sl, :], in_=xsrc[tb * 128:(tb + 1) * 128, :]),
                          wr=[B_xst[sl]])
                if use_junk:
                    P.run("act", lambda e, sl=sl, tb=tb: e.activation(out=junkv, in_=xst[:, sl, :], func=AF.Square,
                                                                     accum_out=stat[:, tb:tb + 1]),
                          rd=[B_xst[sl]], wr=[B_junk, B_stat_p])
                else:
                    P.run("act", lambda e, sl=sl, tb=tb: e.activation(out=xn[:, sl, :], in_=xst[:, sl, :],
                                                                     func=AF.Square, accum_out=stat[:, tb:tb + 1]),
                          rd=[B_xst[sl]], wr=[B_xn[sl], B_stat_p])
                P.run("act", lambda e, tb=tb: e.activation(out=stat[:, 4 + tb:5 + tb], in_=stat[:, tb:tb + 1],
                                                          func=AF.Sqrt, scale=1.0 / D, bias=eps_ap),
                      rd=[B_const], wr=[B_stat_p])
                P.run("dve", lambda e, tb=tb: e.reciprocal(out=stat[:, 8 + tb:9 + tb], in_=stat[:, 4 + tb:5 + tb]),
                      wr=[B_stat_p])
                P.run("dve", lambda e, sl=sl, tb=tb: e.scalar_tensor_tensor(
                    out=xn[:, sl, :], in0=xst[:, sl, :], scalar=stat[:, 8 + tb:9 + tb], in1=gain_bc,
                    op0=ALU.mult, op1=ALU.mult), rd=[B_xst[sl], B_gain, B_stat_p], wr=[B_xn[sl]])

            def pro_stage2(tb):
                sl = tb % 2
                for g8 in range(8):
                    b = alloc_bank()
                    bv = bank(b).bitcast(BF16)
                    fns = []
                    for i in range(4):
                        kc = g8 * 4 + i
                        fns.append(lambda e, bv=bv, i=i, kc=kc, sl=sl: e.transpose(
                            out=bv[:, i * 128:(i + 1) * 128], in_=xn[:, sl, kc * 128:(kc + 1) * 128],
                            identity=ident_bf[:, :]))
                    P.pe_group(fns, rd=[B_xn[sl], B_ident], wr=[B_bank[b]])
                    eng = "act" if g8 % 2 == 0 else "dve"
                    src = bv[:, 0:512].rearrange("p (a b) -> p a b", a=4)
                    dst = hT[:, g8 * 4:(g8 + 1) * 4, tb * 128:(tb + 1) * 128]
                    if eng == "act":
                        hT_last["act"] = P.run("act", lambda e, s_=src, d_=dst: e.activation(out=d_, in_=s_, func=AF.Copy),
                                               rd=[B_bank[b]], extra=hT_inh)
                    else:
                        hT_last["dve"] = P.run("dve", lambda e, s_=src, d_=dst: e.tensor_copy(out=d_, in_=s_),
                                               rd=[B_bank[b]], extra=hT_inh)


            pro_stage1(0)
            pro_stage1(1)
            pro_stage2(0)
            pro_stage1(2)
            pro_stage2(1)
            pro_stage1(3)
            pro_stage2(2)
            pro_stage2(3)
            B_hTa.w = hT_last["act"]
            B_hTd.w = hT_last["dve"]

            def rhs_h(N0, N1):
                return lambda kc: hT[:, kc, N0:N1]

            B_KT = [Buf(B_xst) for _ in range(16)]
            B_V = [Buf(B_xst) for _ in range(4)]
            KTc, Vc = KT(kcur), VV(kcur)

            def evac_k(G):
                def f(ci, b):
                    h = G * 4 + ci
                    P.run("act", lambda e: e.activation(out=KTc[:, h, :], in_=bank(b), func=AF.Copy),
                          rd=[B_bank[b]], wr=[B_KT[h]])
                return f

            def evac_v(G):
                def f(tb, b):
                    eng = "act" if tb % 2 == 0 else "dve"
                    dst = Vc[:, tb, G * 512:(G + 1) * 512]
                    if eng == "act":
                        P.run("act", lambda e: e.activation(out=dst, in_=bank(b), func=AF.Copy),
                              rd=[B_bank[b]], wr=[B_V[G]])
                    else:
                        P.run("dve", lambda e: e.tensor_copy(out=dst, in_=bank(b)), rd=[B_bank[b]], wr=[B_V[G]])
                return f

            if halo:
                P.run("act", lambda e: e.activation(out=httail[:, :, :], in_=hT[:, :, 496:512], func=AF.Copy),
                      rd=B_hTl, wr=[B_httail])
                for G in range(4):
                    proj_ws(w_in_v, C_K + G * 512, 4, rhs_h(0, 512), B_hTl, 512, evac_k(G))
                    proj_wm(w_in_v, C_V + G * 512, lambda kc, tb: hT[:, kc, tb * 128:(tb + 1) * 128], B_hTl,
                            evac_v(G))
                state["regA_prev"] = {"hT": B_hTl, "ya": [B_gain], "yp": [B_xn[0], B_xn[1]]}
                nxt0 = XST(1 - kcur)
                B_pf0 = [Buf(), Buf()]
                for tbn in range(2):
                    P.dma("sp", "xs%d" % tbn,
                          lambda e, tbn=tbn: e.dma_start(out=nxt0[:, tbn, :], in_=x_own[tbn * 128:(tbn + 1) * 128, :]),
                          wr=[B_pf0[tbn]])
                state["xpf"] = B_pf0
                state["regC_prev"] = inhC
                state["R_prev_bufs"][kcur] = []
                state["KV"] = (B_KT, B_V, kcur)
                return

            B_KTp, B_Vp, kp_chk = state["KV"]
            assert kp_chk == kprev
            KTp, Vp = KT(kprev), VV(kprev)

            Ub = [regC[:, 0:528], regC[:, 528:1056]]
            T1 = regC[:, 1056:1584]
            T2 = regC[:, 1584:2112]
            dT = regC[:, 2112:4160].bitcast(BF16).rearrange("p (g a b) -> p g a b", g=2, a=4)
            thz = regC[:, 4160:5184].rearrange("p (a b) -> p a b", a=2)
            B_U = [Buf(inhC), Buf(inhC)]
            B_T1, B_T2 = Buf(inhC), Buf(inhC)
            B_dT = [Buf(inhC), Buf(inhC)]
            B_thz = [Buf(inhC), Buf(inhC)]
            B_yp = [Buf([B_xn[0], B_xn[1]]) for _ in range(16)]
            B_ya = [Buf([B_gain]) for _ in range(16)]
            ucount = {"n": 0}

            def evac_u(g):
                def f(ci, b):
                    c = g * 4 + ci
                    w = POOL_W[g]
                    ub = ucount["n"] % 2
                    ucount["n"] += 1
                    U = Ub[ub]
                    P.run("act", lambda e: e.activation(out=U[:, 16:528], in_=bank(b), func=AF.Copy),
                          rd=[B_bank[b]], wr=[B_U[ub]])
                    P.run("act", lambda e: e.activation(out=U[:, 0:16], in_=tails[:, c, :], func=AF.Copy),
                          rd=[B_tails], wr=[B_U[ub]])
                    P.run("act", lambda e: e.activation(out=tails[:, c, :], in_=U[:, 512:528], func=AF.Copy),
                          rd=[B_U[ub]], wr=[B_tails])
                    src, Bsrc = U, B_U[ub]
                    steps = {2: [1], 4: [1, 2], 8: [1, 2, 4], 16: [1, 2, 4, 8]}[w]
                    lo = 0
                    tgt = [(T1, B_T1), (T2, B_T2)]
                    for si, sh in enumerate(steps):
                        dst, Bdst = tgt[si % 2]
                        lo2 = lo + sh
                        P.run("dve", lambda e, dst=dst, src=src, lo2=lo2, sh=sh: e.tensor_tensor(
                            out=dst[:, lo2:528], in0=src[:, lo2:528], in1=src[:, lo2 - sh:528 - sh], op=ALU.add),
                            rd=[Bsrc], wr=[Bdst])
                        src, Bsrc, lo = dst, Bdst, lo2
                    db = g % 2
                    P.run("dve", lambda e, src=src: e.scalar_tensor_tensor(
                        out=dT[:, db, ci, :], in0=src[:, 16:528], scalar=1.0 / w, in1=U[:, 16:528],
                        op0=ALU.mult, op1=ALU.subtract), rd=[Bsrc, B_U[ub]], wr=[B_dT[db]])
                    if ti == 0:
                        P.run("dve", lambda e, src=src: e.tensor_tensor(
                            out=T1[:, 0:16] if src is not T1 else T2[:, 0:16], in0=src[:, 16:32],
                            in1=invc[:, g * 16:(g + 1) * 16], op=ALU.mult),
                            rd=[Bsrc, B_const], wr=[B_T1 if src is not T1 else B_T2])
                        tmp = T1 if src is not T1 else T2
                        Btmp = B_T1 if src is not T1 else B_T2
                        P.run("dve", lambda e, tmp=tmp: e.tensor_tensor(
                            out=dT[:, db, ci, 0:16], in0=tmp[:, 0:16], in1=U[:, 16:32], op=ALU.subtract),
                            rd=[Btmp, B_U[ub]], wr=[B_dT[db]])
                return f

            def pool_group(g):
                db = g % 2
                zb = {}

                def evac_z(ci, b):
                    zb[ci] = b
                proj_ws(w_in_v, C_ZP + g * 512, 4, rhs_h(0, 512), B_hTl, 512, evac_z)
                s = load_slab(pool_w_v[:, g * 4:(g + 1) * 4, 0:512], True)
                sv = slot8(s)
                for ci in range(4):
                    c = g * 4 + ci
                    tz = ci % 2
                    pb = alloc_bank()
                    fns = [mm(bank(pb), sv[:, k, ci * 128:(ci + 1) * 128], dT[:, db, k, :], k == 0, k == 3)
                           for k in range(4)]
                    P.pe_group(fns, rd=[B_slot[s], B_dT[db]], wr=[B_bank[pb]])
                    z = zb[ci]
                    P.run("act", lambda e, z=z, tz=tz: e.activation(out=thz[:, tz, :], in_=bank(z), func=AF.Tanh,
                                                                   scale=0.5), rd=[B_bank[z]], wr=[B_thz[tz]])
                    P.run("dve", lambda e, z=z, tz=tz: e.scalar_tensor_tensor(
                        out=thz[:, tz, :], in0=thz[:, tz, :], scalar=1.0, in1=bank(z), op0=ALU.add, op1=ALU.mult),
                        rd=[B_bank[z]], wr=[B_thz[tz]])
                    P.run("dve", lambda e, pb=pb, tz=tz, c=c: e.scalar_tensor_tensor(
                        out=ypT[:, c, :], in0=bank(pb), scalar=psc[:, c:c + 1], in1=thz[:, tz, :],
                        op0=ALU.mult, op1=ALU.mult), rd=[B_bank[pb], B_thz[tz], B_const], wr=[B_yp[c]])

            def u_proj(g):
                if ti == 0:
                    tbk = alloc_bank()
                    P.run("dve", lambda e: e.memset(bank(tbk)[:, 0:64], 0.0), wr=[B_bank[tbk]])
                    ev = evac_u(g)

                    def ev2(ci, b):
                        if ci == 0:
                            P.run("act", lambda e: e.activation(
                                out=tails[:, g * 4:(g + 1) * 4, :],
                                in_=bank(tbk)[:, 0:64].rearrange("p (a b) -> p a b", a=4), func=AF.Copy),
                                rd=[B_bank[tbk]], wr=[B_tails])
                        ev(ci, b)
                    proj_ws(w_in_v, C_U + g * 512, 4, rhs_h(0, 512), B_hTl, 512, ev2, tail_bank=tbk)
                else:
                    proj_ws(w_in_v, C_U + g * 512, 4, rhs_h(0, 512), B_hTl, 512, evac_u(g))

            u_proj(0)
            for g in range(4):
                if g + 1 < 4:
                    u_proj(g + 1)
                pool_group(g)

            poolC = B_U + [B_T1, B_T2] + B_dT + B_thz

            qT2 = regC[:, 0:2048].bitcast(BF16).rearrange("p (g a b) -> p g a b", g=2, a=4)
            Xb = regC[:, 2048:2560].rearrange("p (a b) -> p a b", a=2)
            PT = regC[:, 2560:3200].bitcast(BF16).rearrange("p (a b) -> p a b", a=2)
            tha = regC[:, 3200:5248].rearrange("p (a b) -> p a b", a=4)
            rden = regC[:, 5248:5760]
            tt = rden
            B_qT2 = [[Buf(poolC) for _ in range(4)] for _ in range(2)]
            B_X = [Buf(poolC), Buf(poolC)]
            B_PT = [Buf(poolC), Buf(poolC)]
            B_sz = [Buf(poolC) for _ in range(4)]
            B_rq = [Buf(poolC) for _ in range(4)]
            ucnt = {"n": 0}

            def evac_q_of(G):
                def f(ci, b):
                    P.run("act", lambda e: e.activation(out=qT2[:, G % 2, ci, :], in_=bank(b), func=AF.Copy),
                          rd=[B_bank[b]], wr=[B_qT2[G % 2][ci]])
                return f

            def unit_S(h, hl, qb, sb=None):
                u = ucnt["n"] % 2
                ucnt["n"] += 1
                if sb is None:
                    a = alloc_bank(range(4))
                    bq = alloc_bank(range(4))
                    ee = "pool"
                else:
                    a, bq = sb
                    ee = "dve"
                qT = qT2[:, (h // 4) % 2, :, :]
                fns, rd = [], [B_qT2[(h // 4) % 2][hl]]
                for (bk, col, j) in ((a, 0, 0), (a, 1, 1), (a, 2, 2), (bq, 0, 3), (bq, 1, 4)):
                    r = qb + j
                    src, Bs = (KTp, B_KTp) if r < 4 else (KTc, B_KT)
                    kb = r % 4
                    fns.append(mm(bank(bk)[:, col * 128:(col + 1) * 128], src[:, h, kb * 128:(kb + 1) * 128],
                                  qT[:, hl, qb * 128:(qb + 1) * 128], True, True))
                    rd.append(Bs[h])
                P.pe_group(fns, rd=rd, wr=[B_bank[a], B_bank[bq]])
                P.run("act", lambda e: e.activation(out=PT[:, u, 0:384], in_=bank(a)[:, 0:384], func=AF.Exp,
                                                    scale=SCALE, bias=bfar[:, h:h + 1]),
                      rd=[B_bank[a], B_const], wr=[B_PT[u]])
                P.run(ee, lambda e: e.memset(PT[0:64, u, 64:128], 0.0), wr=[B_PT[u]])
                P.run("act", lambda e: e.activation(out=Xb[:, u, :], in_=bank(bq)[:, 0:256], func=AF.Exp, scale=SCALE),
                      rd=[B_bank[bq]], wr=[B_X[u]])
                P.run(ee, lambda e: e.tensor_tensor(out=PT[:, u, 384:640], in0=Xb[:, u, :], in1=etab[:, h, :],
                                                    op=ALU.mult), rd=[B_X[u], B_const], wr=[B_PT[u]])
                return u

            def unit_PV(h, qb, u, ob, db_):
                fns, rd = [], [B_PT[u], B_const]
                order = (0, 1, 2, 3, 4)
                for bi, j in enumerate(order):
                    r = qb + j
                    src, Bs = (Vp, B_Vp) if r < 4 else (Vc, B_V)
                    kb = r % 4
                    pt = PT[:, u, bi * 128:(bi + 1) * 128]
                    fns.append(mm(bank(ob)[:, qb * 128:(qb + 1) * 128], src[:, kb, h * 128:(h + 1) * 128], pt,
                                  bi == 0, bi == 4))
                    one = hv_bf if (ti == 0 and r < 4) else ones_bf
                    fns.append(mm(bank(db_)[:, qb * 128:(qb + 1) * 128], one[:, :], pt, bi == 0, bi == 4))
                    rd.append(Bs[h // 4])
                P.pe_group(fns, rd=rd, wr=[B_bank[ob], B_bank[db_]])

            def head_evac(h, ci, ob, db_, ee="pool"):
                P.run("dve", lambda e: e.reciprocal(out=rden, in_=bank(db_)), rd=[B_bank[db_]], wr=B_rq)
                P.run(ee, lambda e: e.tensor_tensor(out=rden, in0=rden, in1=tha[:, ci, :], op=ALU.mult),
                      rd=[B_sz[ci]], wr=B_rq)
                P.run("dve", lambda e: e.tensor_tensor(out=yaT[:, h, :], in0=bank(ob), in1=rden, op=ALU.mult),
                      rd=[B_bank[ob]] + B_rq, wr=[B_ya[h]])

            def unit_evac(h, ci, qb, ob, db_):
                cs = slice(qb * 128, (qb + 1) * 128)
                P.run("dve", lambda e: e.reciprocal(out=rden[:, cs], in_=bank(db_)[:, cs]),
                      rd=[B_bank[db_]], wr=[B_rq[qb]])
                P.run("dve", lambda e: e.tensor_tensor(out=rden[:, cs], in0=rden[:, cs], in1=tha[:, ci, cs],
                                                       op=ALU.mult), rd=[B_sz[ci]], wr=[B_rq[qb]])
                P.run("dve", lambda e: e.tensor_tensor(out=yaT[:, h, cs], in0=bank(ob)[:, cs], in1=rden[:, cs],
                                                       op=ALU.mult), rd=[B_bank[ob], B_rq[qb]], wr=[B_ya[h]])

            def A_qkv(G, filler, banks):
                proj_ws(w_in_v, C_Q + G * 512, 4, rhs_h(0, 512), B_hTl, 512, evac_q_of(G), filler=filler, banks=banks)
                proj_ws(w_in_v, C_K + G * 512, 4, rhs_h(0, 512), B_hTl, 512, evac_k(G), filler=filler, banks=banks)
                proj_wm(w_in_v, C_V + G * 512, lambda kc, tb: hT[:, kc, tb * 128:(tb + 1) * 128], B_hTl, evac_v(G),
                        filler=filler, banks=banks)

            def Zq(G):
                zb = {}

                def evac_z(ci, b):
                    zb[ci] = b
                proj_ws(w_in_v, C_ZA + G * 512, 4, rhs_h(0, 512), B_hTl, 512, evac_z)
                for ci in range(4):
                    z = zb[ci]
                    P.run("act", lambda e, z=z, ci=ci: e.activation(out=tha[:, ci, :], in_=bank(z), func=AF.Tanh,
                                                                   scale=0.5), rd=[B_bank[z]], wr=[B_sz[ci]])
                    P.run("dve", lambda e, z=z, ci=ci: e.scalar_tensor_tensor(
                        out=tha[:, ci, :], in0=tha[:, ci, :], scalar=1.0, in1=bank(z), op0=ALU.add, op1=ALU.mult),
                        rd=[B_bank[z]], wr=[B_sz[ci]])

            def unit_steps(G):
                h0 = G * 4
                units = [(ci, qb) for ci in range(4) for qb in range(4)]
                prev = None
                for nxt in units + [None]:
                    if prev is not None:
                        pci, pqb, pu = prev
                        unit_PV(h0 + pci, pqb, pu, 6, 7)
                        unit_evac(h0 + pci, pci, pqb, 6, 7)
                    if nxt is not None:
                        ci, qb = nxt
                        u = unit_S(h0 + ci, ci, qb, sb=(4, 5))
                        prev = (ci, qb, u)
                    yield 0

            A_qkv(0, None, range(8))
            Zq(0)
            for G in range(4):
                if G < 3:
                    gen = unit_steps(G)
                    A_qkv(G + 1, lambda gen=gen: next(gen, None), range(4))
                    for _ in gen:
                        pass
                    Zq(G + 1)
                else:
                    preload_quad(w_in_v, C_GA, 4)
                    units = [(ci, qb) for ci in range(4) for qb in range(4)]
                    obank = {0: (4, 5), 1: (6, 7)}
                    us = {}
                    h0 = G * 4
                    us[0] = unit_S(h0 + units[0][0], units[0][0], units[0][1])
                    for i, (ci, qb) in enumerate(units):
                        if i + 1 < len(units):
                            ci2, qb2 = units[i + 1]
                            us[i + 1] = unit_S(h0 + ci2, ci2, qb2)
                        ob, db_ = 4 + (qb % 2), 6 + (qb % 2)
                        unit_PV(h0 + ci, qb, us[i], ob, db_)
                        unit_evac(h0 + ci, ci, qb, ob, db_)

            attnC = B_qT2[0] + B_qT2[1] + B_X + B_PT + B_sz + B_rq
            assert st["tslab"] == SLABS_P12 + len(pre_q), st["tslab"]

            mT = MT(kprev)
            inh_m = list(B_KTp) + list(B_Vp)
            B_m = [Buf(inh_m) for _ in range(32)]
            th3 = regC[:, 0:4096].rearrange("p (x a b) -> p x a b", x=2, a=4)
            B_th3 = [[Buf(attnC) for _ in range(4)], [Buf(attnC) for _ in range(4)]]
            for jq in range(8):
                for br in range(2):
                    gcol = (C_GA if br == 0 else C_GP) + jq * 512
                    wo_v = w_oa_v if br == 0 else w_op_v
                    yT = yaT if br == 0 else ypT
                    B_y = B_ya if br == 0 else B_yp
                    gbk = {}

                    def evac_g(ci, b):
                        gbk[ci] = b
                    proj_ws(w_in_v, gcol, 4, rhs_h(0, 512), B_hTl, 512, evac_g)
                    obk = {}

                    def evac_o(ci, b):
                        obk[ci] = b
                    proj_ws(wo_v, jq * 512, 2, lambda kc, yT=yT: yT[:, kc, :], B_y, 512, evac_o)
                    for ci in range(4):
                        c = jq * 4 + ci
                        gb_, ob_ = gbk[ci], obk[ci]
                        P.run("act", lambda e, gb_=gb_, ci=ci, c=c, br=br: e.activation(
                            out=th3[:, br, ci, :], in_=bank(gb_), func=AF.Tanh, scale=0.5,
                            bias=hb[:, br * 32 + c: br * 32 + c + 1]),
                            rd=[B_bank[gb_], B_const], wr=[B_th3[br][ci]])
                        P.run("dve", lambda e, ob_=ob_, ci=ci, br=br: e.scalar_tensor_tensor(
                            out=th3[:, br, ci, :], in0=th3[:, br, ci, :], scalar=1.0, in1=bank(ob_),
                            op0=ALU.add, op1=ALU.mult), rd=[B_bank[ob_]], wr=[B_th3[br][ci]])
                        if br == 1:
                            P.run("dve", lambda e, ci=ci, c=c: e.tensor_tensor(
                                out=mT[:, c, :], in0=th3[:, 0, ci, :], in1=th3[:, 1, ci, :], op=ALU.add),
                                rd=[B_th3[0][ci], B_th3[1][ci]], wr=[B_m[c]])

            ph3C = B_th3[0] + B_th3[1]

            xs4 = regC[:, 0:4096].rearrange("p (s a b) -> p s a b", s=2, a=4)
            fgE = regC[:, 4096:5120].rearrange("p (a b) -> p a b", a=2)
            B_xs4 = [Buf(ph3C), Buf(ph3C)]
            B_fgE = [Buf(ph3C), Buf(ph3C)]
            inh_o = B_hTl + B_ya + B_yp
            B_ob = [Buf(inh_o) for _ in range(4)]
            xo = x_own[ti * T:(ti + 1) * T, :].rearrange("(a p) c -> p a c", p=128)
            oo = out[ti * T:(ti + 1) * T, :].rearrange("(a p) c -> p a c", p=128)
            for eb in range(8):
                sl = eb % 2
                P.dma("sp", "x4%d" % sl,
                      lambda e, sl=sl, eb=eb: e.dma_start(out=xs4[:, sl, :, :], in_=xo[:, :, eb * 512:(eb + 1) * 512]),
                      wr=[B_xs4[sl]])
                P.dma("sp", "fg%d" % sl,
                      lambda e, sl=sl, eb=eb: e.dma_start(out=fgE[:, sl, :],
                                                          in_=fg[eb * 512:(eb + 1) * 512].partition_broadcast(128)),
                      wr=[B_fgE[sl]])

                def evac_o4(tb, b, eb=eb, sl=sl):
                    dst = outb[:, tb, eb * 512:(eb + 1) * 512]
                    P.run("dve", lambda e: e.scalar_tensor_tensor(out=dst, in0=bank(b), scalar=0.25,
                                                                 in1=xs4[:, sl, tb, :], op0=ALU.mult, op1=ALU.add),
                          rd=[B_bank[b], B_xs4[sl]], wr=[B_ob[tb]])
                    P.run("act", lambda e: e.activation(out=bank(b), in_=dst, func=AF.Square,
                                                        accum_out=stat[:, 32 + tb * 8 + eb: 33 + tb * 8 + eb]),
                          rd=[B_ob[tb]], wr=[B_bank[b], B_stat])
                    P.run("dve", lambda e: e.tensor_tensor(out=dst, in0=dst, in1=fgE[:, sl, :], op=ALU.mult),
                          rd=[B_fgE[sl]], wr=[B_ob[tb]])
                proj_wm(w_out_v, eb * 512, lambda kc, tb: mT[:, kc, tb * 128:(tb + 1) * 128], B_m, evac_o4)
            if ti + 1 < NTILE:
                nxt = XST(kprev)
                xs_n = x_own[(ti + 1) * T:(ti + 2) * T, :]
                B_pf = [Buf(B_m), Buf(B_m)]
                for tbn in range(2):
                    P.dma("sp", "xs%d" % tbn,
                          lambda e, tbn=tbn: e.dma_start(out=nxt[:, tbn, :], in_=xs_n[tbn * 128:(tbn + 1) * 128, :]),
                          wr=[B_pf[tbn]])
                state["xpf"] = B_pf
            for tb in (2, 3, 0, 1):
                P.run("dve", lambda e, tb=tb: e.tensor_reduce(out=stat[:, 16 + tb:17 + tb],
                                                             in_=stat[:, 32 + tb * 8: 40 + tb * 8],
                                                             axis=mybir.AxisListType.X, op=ALU.add),
                      wr=[B_stat])
                P.run("act", lambda e, tb=tb: e.activation(out=stat[:, 20 + tb:21 + tb], in_=stat[:, 16 + tb:17 + tb],
                                                          func=AF.Sqrt, scale=1.0 / D, bias=eps_ap),
                      rd=[B_const], wr=[B_stat])
                P.run("dve", lambda e, tb=tb: e.reciprocal(out=stat[:, 24 + tb:25 + tb], in_=stat[:, 20 + tb:21 + tb]),
                      wr=[B_stat])
            for tb in (2, 3, 0, 1):
                if tb in (2, 0):
                    P.run("dve", lambda e, tb=tb: e.tensor_scalar(
                        out=outb[:, tb, :], in0=outb[:, tb, :], scalar1=stat[:, 24 + tb:25 + tb], scalar2=None,
                        op0=ALU.mult), rd=[B_stat], wr=[B_ob[tb]])
                else:
                    P.run("act", lambda e, tb=tb: e.activation(
                        out=outb[:, tb, :], in_=outb[:, tb, :], func=AF.Copy, scale=stat[:, 24 + tb:25 + tb]),
                        rd=[B_stat], wr=[B_ob[tb]])
                P.dma("sp", "out%d" % tb, lambda e, tb=tb: e.dma_start(out=oo[:, tb, :], in_=outb[:, tb, :]),
                      rd=[B_ob[tb]])

            assert st["tslab"] == SLABS_TILE, st["tslab"]
            state["regA_prev"] = {"hT": [B_ob[0], B_ob[1]], "ya": [B_ob[2]], "yp": [B_ob[3]]}
            state["regC_prev"] = B_xs4 + B_fgE
            state["R_prev_bufs"][kprev] = list(B_m)
            state["R_prev_bufs"][kcur] = []
            state["KV"] = (B_KT, B_V, kcur)
            state["last_ob"] = B_ob

        do_tile(-1)
        for ti in range(NTILE):
            do_tile(ti)
        for tb in range(4):
            P.s["sp"].wait(("out%d" % tb, 16 * P.dma_count["out%d" % tb]))

        def replay(stream, eng):
            for op in stream.ops:
                if op[0] == "wait":
                    eng.wait_ge(sems[op[1]], op[2])
                else:
                    ins = op[1](eng)
                    if op[2] is not None:
                        ins.then_inc(sems[op[2]], op[3])

        @block.tensor
        def _(e):
            replay(P.s["pe"], e)

        @block.scalar
        def _(e):
            replay(P.s["act"], e)

        @block.vector
        def _(e):
            replay(P.s["dve"], e)

        @block.gpsimd
        def _(e):
            replay(P.s["pool"], e)

        @block.sync
        def _(e):
            replay(P.s["sp"], e)

    return nc


_NC_CACHE = {}


def _host_consts(rel_bias):
    ki = np.arange(128)[:, None]
    qi = np.arange(128)[None, :]
    idx3 = np.minimum(qi - ki, 0) + 256
    idx4 = qi - ki + 128
    eb = np.empty((16, 128, 256), np.float32)
    eb[:, :, 0:128] = rel_bias[:, idx3]
    eb[:, :, 128:256] = rel_bias[:, idx4]
    em = np.ones((128, 256), np.float32)
    em[64:128, 128:192] = 0.0
    bfar = np.ascontiguousarray(np.broadcast_to(rel_bias[:, 256][None, :], (128, 16))).astype(np.float32)
    return eb, em, bfar


def kernel(x, norm_gain, w_in, rel_bias, pool_w, pool_scale, w_out_attn, w_out_pool, gate_bias, w_out,
           final_gain):
    x = np.asarray(x, np.float32)
    f = lambda a: np.ascontiguousarray(np.asarray(a, np.float32))
    w_in, w_oa, w_op, w_o = f(w_in), f(w_out_attn), f(w_out_pool), f(w_out)
    pw = f(pool_w).reshape(2048, 512)
    ng, fgn = f(norm_gain), f(final_gain)
    gb = f(gate_bias)
    gb_fm = np.ascontiguousarray(gb.reshape(2, 32, 128).transpose(2, 0, 1).reshape(128, 64))
    ps_fm = np.ascontiguousarray(f(pool_scale).reshape(16, 128).T)
    eb, em, bfar = _host_consts(f(rel_bias))
    ident = np.eye(128, dtype=np.float32)

    if "nc" not in _NC_CACHE:
        _NC_CACHE["nc"] = build_program()
    nc = _NC_CACHE["nc"]

    in_maps = []
    for c in range(8):
        b, half = c // 2, c % 2
        own = np.ascontiguousarray(x[b, half * OWN:(half + 1) * OWN, :])
        if half == 0:
            halo = np.zeros((T, D), np.float32)
            hv = np.zeros((128, 128), np.float32)
            invc = np.empty((128, 64), np.float32)
            for g, w in enumerate(POOL_W):
                invc[:, g * 16:(g + 1) * 16] = (1.0 / np.minimum(np.arange(16) + 1, w)).astype(np.float32)[None, :]
        else:
            halo = np.ascontiguousarray(x[b, OWN - T:OWN, :])
            hv = np.ones((128, 128), np.float32)
            invc = np.empty((128, 64), np.float32)
            for g, w in enumerate(POOL_W):
                invc[:, g * 16:(g + 1) * 16] = np.float32(1.0 / w)
        in_maps.append({
            "x_own": own, "x_halo": halo, "w_in": w_in, "pool_w": pw, "w_oa": w_oa, "w_op": w_op, "w_out": w_o,
            "ng": ng, "fg": fgn, "gb_fm": gb_fm, "ps_fm": ps_fm, "ebias": eb, "emask": em, "bfar": bfar,
            "hv": hv, "invc": invc, "ident": ident,
        })
    res = run_bass_kernel_spmd(nc, in_maps, core_ids=list(range(8)))
    outp = np.empty((4, 4096, 4096), np.float32)
    for c in range(8):
        b, half = c // 2, c % 2
        outp[b, half * OWN:(half + 1) * OWN, :] = res.results[c]["out"]
    return outp
```
